# Optimizing a Trainium2 kernel written in Bass

```python
import jax, jax.numpy as jnp
from jax import lax
import numpy as np

D_MODEL = 1024
BATCH = 32
SEQ = 256
DEPTH = 2
DEC_BATCH = 4
DEC_SEQ = 4096
PAST_LEN = 512

GRID_W = 64
N_EVEN = (DEPTH + 1) // 2
N_ODD = DEPTH // 2
ATT_HEADS = 8
ATT_KV_HEADS = 2
HEAD_DIM = 64
ATT_GROUP = ATT_HEADS // ATT_KV_HEADS
ATT_W = ATT_HEADS * HEAD_DIM
KV_W = ATT_KV_HEADS * HEAD_DIM
ROPE_AXIS_DIM = HEAD_DIM // 2
ROPE_THETA = 10000.0
Q_BLOCK = 128
CONV_CH = D_MODEL // 2
CONV_K = 31
EVEN_IN = ATT_W + 2 * KV_W + 2 * CONV_CH
MIX_W = ATT_W + CONV_CH
M_HEADS = 4
M_INNER = D_MODEL
M_DK = M_INNER // M_HEADS
M_DV = M_INNER // M_HEADS
M_CHUNK = 64
ODD_IN = 4 * M_INNER + 4 * M_HEADS
D_FF = 4 * D_MODEL
ALPHA = (2 * DEPTH) ** 0.25
BETA = (8 * DEPTH) ** -0.25
EPS = 1e-6

kernel_name = "hybrid_diffusion_ctx_prefix_step"


def _layernorm(x, g, b):
    xf = x.astype(jnp.float32)
    mu = jnp.mean(xf, axis=-1, keepdims=True)
    var = jnp.mean(jnp.square(xf - mu), axis=-1, keepdims=True)
    y = (xf - mu) * lax.rsqrt(var + EPS) * g.astype(jnp.float32) + b.astype(jnp.float32)
    return y.astype(x.dtype)


def _rmsnorm(x, g):
    xf = x.astype(jnp.float32)
    y = xf * lax.rsqrt(jnp.mean(xf * xf, axis=-1, keepdims=True) + EPS) * g.astype(jnp.float32)
    return y.astype(x.dtype)


def _adaln(cond, w, b):
    m = jax.nn.silu(cond) @ w + b
    return jnp.split(m[:, None, :], 6, axis=-1)


def _modulate(x, shift, scale):
    return x * (1 + scale) + shift


def _residual_norm(x, y, gate, g, b):
    return _layernorm(ALPHA * x + gate * y, g, b)


def _ffn(h, w1, w2):
    return jnp.square(jax.nn.relu(h @ w1)) @ w2


def _axial_rope(n):
    rows = n // GRID_W
    row = jnp.repeat(jnp.arange(rows), GRID_W).astype(jnp.float32)
    col = (jnp.arange(n) % GRID_W).astype(jnp.float32)
    freqs = ROPE_THETA ** (-jnp.arange(0, ROPE_AXIS_DIM, 2, dtype=jnp.float32) / ROPE_AXIS_DIM)
    ang = jnp.concatenate([row[:, None] * freqs, col[:, None] * freqs], axis=-1)
    return jnp.cos(ang)[:, None, :], jnp.sin(ang)[:, None, :]


def _apply_rope(x, cos, sin):
    xf = x.astype(jnp.float32).reshape(*x.shape[:-1], HEAD_DIM // 2, 2)
    x0, x1 = xf[..., 0], xf[..., 1]
    out = jnp.stack([x0 * cos - x1 * sin, x0 * sin + x1 * cos], axis=-1)
    return out.reshape(x.shape).astype(x.dtype)


def _block_attention(q, k, v):
    b, sq = q.shape[0], q.shape[1]
    nb = sq // Q_BLOCK
    qb = q.reshape(b, nb, Q_BLOCK, ATT_KV_HEADS, ATT_GROUP, HEAD_DIM).transpose(1, 0, 2, 3, 4, 5)
    scale = HEAD_DIM ** -0.5

    def one_block(qi):
        s = jnp.einsum('bqkgd,bskd->bkgqs', qi, k).astype(jnp.float32) * scale
        p = jax.nn.softmax(s, axis=-1).astype(v.dtype)
        return jnp.einsum('bkgqs,bskd->bqkgd', p, v)

    o = lax.map(one_block, qb)
    return o.transpose(1, 0, 2, 3, 4, 5).reshape(b, sq, ATT_W)


def _depthwise_conv(u, w, bias):
    y = lax.conv_general_dilated(
        u, w[:, None, :].astype(u.dtype), window_strides=(1,),
        padding=[(CONV_K // 2, CONV_K // 2)], dimension_numbers=('NWC', 'WIO', 'NWC'),
        feature_group_count=CONV_CH)
    return y + bias


def _attn_conv_mixer(h, w_in, q_gain, k_gain, conv_w, conv_b, cln_g, cln_b, w_out, ctx_k=None, ctx_v=None):
    b, s, _ = h.shape
    proj = h @ w_in
    q, k, v, u = jnp.split(proj, [ATT_W, ATT_W + KV_W, ATT_W + 2 * KV_W], axis=-1)
    q = _rmsnorm(q.reshape(b, s, ATT_HEADS, HEAD_DIM), q_gain)
    k = _rmsnorm(k.reshape(b, s, ATT_KV_HEADS, HEAD_DIM), k_gain)
    v = v.reshape(b, s, ATT_KV_HEADS, HEAD_DIM)
    if ctx_k is None:
        att = _block_attention(q, k, v)
    else:
        cos, sin = _axial_rope(s)
        q_r = _apply_rope(q, cos, sin)
        k_r = _apply_rope(k, cos, sin)
        k_all = jnp.concatenate([k_r, ctx_k.astype(k.dtype)], axis=1)
        v_all = jnp.concatenate([v, ctx_v.astype(v.dtype)], axis=1)
        att = _block_attention(q_r, k_all, v_all)
    a, gt = jnp.split(u, 2, axis=-1)
    u = a * jax.nn.sigmoid(gt)
    u = _depthwise_conv(u, conv_w, conv_b)
    u = jax.nn.silu(_layernorm(u, cln_g, cln_b))
    out = jnp.concatenate([att, u], axis=-1) @ w_out
    return out, k, v


def _to_chunks(t):
    b, s = t.shape[0], t.shape[1]
    t = t.reshape(b, s // M_CHUNK, M_CHUNK, *t.shape[2:])
    return jnp.moveaxis(jnp.moveaxis(t, 1, 0), 3, 2)


def _mlstm_scan(q, k, v, li, lf, c0, n0, m0):
    b, s = q.shape[0], q.shape[1]
    mask = jnp.tril(jnp.ones((M_CHUNK, M_CHUNK), dtype=bool))

    def step(carry, inp):
        c, n, m = carry
        qc, kc, vc, lic, lfc = inp
        cum = jnp.cumsum(lfc, axis=-1)
        d = cum[..., :, None] - cum[..., None, :] + lic[..., None, :]
        d = jnp.where(mask, d, -jnp.inf)
        a_inter = cum + m[..., None]
        m_t = jnp.maximum(a_inter, jnp.max(d, axis=-1))
        w = jnp.exp(d - m_t[..., None])
        s_inter = jnp.exp(a_inter - m_t)
        qk = jnp.einsum('bhtd,bhsd->bhts', qc, kc) * w
        num = jnp.einsum('bhts,bhsv->bhtv', qk, vc) + s_inter[..., None] * jnp.einsum('bhtd,bhdv->bhtv', qc, c)
        den = jnp.sum(qk, axis=-1) + s_inter * jnp.einsum('bhtd,bhd->bht', qc, n)
        h = num / jnp.maximum(jnp.abs(den), jnp.exp(-m_t))[..., None]
        tot = cum[..., -1]
        g = tot[..., None] - cum + lic
        m_new = jnp.maximum(tot + m, jnp.max(g, axis=-1))
        ws = jnp.exp(g - m_new[..., None])
        decay = jnp.exp(tot + m - m_new)
        kw = kc * ws[..., None]
        c_new = decay[..., None, None] * c + jnp.einsum('bhsd,bhsv->bhdv', kw, vc)
        n_new = decay[..., None] * n + jnp.sum(kw, axis=2)
        return (c_new, n_new, m_new), h

    xs = (_to_chunks(q), _to_chunks(k), _to_chunks(v), _to_chunks(li), _to_chunks(lf))
    (c_t, n_t, m_t), h = lax.scan(step, (c0, n0, m0), xs)
    h = h.transpose(1, 0, 3, 2, 4).reshape(b, s, M_HEADS, M_DV)
    return h, c_t, n_t, m_t


def _mlstm_mixer(h, w_in, b_gate, mh_gain, w_out, st_c, st_n, st_m):
    b, s, _ = h.shape
    proj = h @ w_in
    q, k, v, o, gates = jnp.split(proj, [M_INNER, 2 * M_INNER, 3 * M_INNER, 4 * M_INNER], axis=-1)
    q = q.reshape(b, s, M_HEADS, M_DK).astype(jnp.float32)
    k = k.reshape(b, s, M_HEADS, M_DK).astype(jnp.float32) * (M_DK ** -0.5)
    v = v.reshape(b, s, M_HEADS, M_DV).astype(jnp.float32)
    gates = (gates + b_gate).astype(jnp.float32).reshape(b, s, 4, M_HEADS)
    li_f, lf_f = gates[:, :, 0], jax.nn.log_sigmoid(gates[:, :, 1])
    li_b, lf_b = gates[:, :, 2], jax.nn.log_sigmoid(gates[:, :, 3])
    sc = st_c.astype(jnp.float32)
    sn = st_n.astype(jnp.float32)
    sm = st_m.astype(jnp.float32)
    h_f, cf, nf, mf = _mlstm_scan(q, k, v, li_f, lf_f, sc[:, 0], sn[:, 0], sm[:, 0])
    fl = lambda t: jnp.flip(t, axis=1)
    h_b, cb, nb, mb = _mlstm_scan(fl(q), fl(k), fl(v), fl(li_b), fl(lf_b), sc[:, 1], sn[:, 1], sm[:, 1])
    h_sum = h_f + fl(h_b)
    hn = _rmsnorm(h_sum, mh_gain.reshape(M_HEADS, M_DV)).reshape(b, s, M_INNER)
    out = (jax.nn.sigmoid(o.astype(jnp.float32)) * hn).astype(h.dtype) @ w_out
    return out, jnp.stack([cf, cb], axis=1), jnp.stack([nf, nb], axis=1), jnp.stack([mf, mb], axis=1)


def setup_inputs(seed: int = 0) -> dict:
    key = jax.random.key(seed)
    ks = jax.random.split(key, 28)
    f32 = jnp.float32

    def nrm(k, shape, s=1.0):
        return jax.random.normal(k, shape, f32) * s

    lin = jnp.linspace(3.0, 6.0, M_HEADS, dtype=f32)
    zh = jnp.zeros((M_HEADS,), f32)
    gate_base = jnp.concatenate([zh, lin, zh, lin])
    return {
        'x_prompt': nrm(ks[0], (BATCH, SEQ, D_MODEL)),
        'x_sample': nrm(ks[1], (DEC_BATCH, DEC_SEQ, D_MODEL)),
        'cache_k': nrm(ks[2], (DEC_BATCH, N_EVEN, PAST_LEN, ATT_KV_HEADS, HEAD_DIM)),
        'cache_v': nrm(ks[3], (DEC_BATCH, N_EVEN, PAST_LEN, ATT_KV_HEADS, HEAD_DIM)),
        'state_c': nrm(ks[4], (DEC_BATCH, N_ODD, 2, M_HEADS, M_DK, M_DV), 0.3),
        'state_n': nrm(ks[5], (DEC_BATCH, N_ODD, 2, M_HEADS, M_DK), 0.3),
        'state_m': nrm(ks[6], (DEC_BATCH, N_ODD, 2, M_HEADS), 0.5),
        'c': nrm(ks[7], (DEC_BATCH, D_MODEL)),
        'c_ctx': nrm(ks[8], (D_MODEL,)),
        'w_ada': nrm(ks[9], (DEPTH, D_MODEL, 6 * D_MODEL), D_MODEL ** -0.5),
        'b_ada': nrm(ks[10], (DEPTH, 6 * D_MODEL), 0.02),
        'ln_g': 1.0 + nrm(ks[11], (DEPTH, 2, D_MODEL), 0.02),
        'ln_b': nrm(ks[12], (DEPTH, 2, D_MODEL), 0.02),
        'w_ff1': nrm(ks[13], (DEPTH, D_MODEL, D_FF), D_MODEL ** -0.5),
        'w_ff2': nrm(ks[14], (DEPTH, D_FF, D_MODEL), BETA * D_FF ** -0.5),
        'w_in_a': nrm(ks[15], (N_EVEN, D_MODEL, EVEN_IN), D_MODEL ** -0.5),
        'q_gain': 1.0 + nrm(ks[16], (N_EVEN, HEAD_DIM), 0.02),
        'k_gain': 1.0 + nrm(ks[17], (N_EVEN, HEAD_DIM), 0.02),
        'conv_w': nrm(ks[18], (N_EVEN, CONV_K, CONV_CH), CONV_K ** -0.5),
        'conv_b': nrm(ks[19], (N_EVEN, CONV_CH), 0.02),
        'conv_ln_g': 1.0 + nrm(ks[20], (N_EVEN, CONV_CH), 0.02),
        'conv_ln_b': nrm(ks[21], (N_EVEN, CONV_CH), 0.02),
        'w_out_a': nrm(ks[22], (N_EVEN, MIX_W, D_MODEL), BETA * MIX_W ** -0.5),
        'w_in_m': nrm(ks[23], (N_ODD, D_MODEL, ODD_IN), D_MODEL ** -0.5),
        'b_gate_m': gate_base + nrm(ks[24], (N_ODD, 4 * M_HEADS), 0.1),
        'mh_gain': 1.0 + nrm(ks[25], (N_ODD, M_INNER), 0.02),
        'w_out_m': nrm(ks[26], (N_ODD, M_INNER, D_MODEL), BETA * M_INNER ** -0.5),
    }


def reference(x_prompt, x_sample, cache_k, cache_v, state_c, state_n, state_m, c, c_ctx,
              w_ada, b_ada, ln_g, ln_b, w_ff1, w_ff2,
              w_in_a, q_gain, k_gain, conv_w, conv_b, conv_ln_g, conv_ln_b, w_out_a,
              w_in_m, b_gate_m, mh_gain, w_out_m):
    xp, xs = x_prompt, x_sample
    new_k, new_v, new_c, new_n, new_m = [], [], [], [], []
    for l in range(DEPTH):
        mod_p = _adaln(c_ctx[None, :], w_ada[l], b_ada[l])
        mod_s = _adaln(c, w_ada[l], b_ada[l])
        hp = _modulate(xp, mod_p[0], mod_p[1])
        hs = _modulate(xs, mod_s[0], mod_s[1])
        if l % 2 == 0:
            e = l // 2
            wa = (w_in_a[e], q_gain[e], k_gain[e], conv_w[e], conv_b[e], conv_ln_g[e], conv_ln_b[e], w_out_a[e])
            yp, kp, vp = _attn_conv_mixer(hp, *wa)
            ys, _, _ = _attn_conv_mixer(hs, *wa, ctx_k=cache_k[:, e], ctx_v=cache_v[:, e])
            new_k.append(kp)
            new_v.append(vp)
        else:
            o = l // 2
            wm = (w_in_m[o], b_gate_m[o], mh_gain[o], w_out_m[o])
            bp = hp.shape[0]
            z_c = jnp.zeros((bp, 2, M_HEADS, M_DK, M_DV), jnp.float32)
            z_n = jnp.zeros((bp, 2, M_HEADS, M_DK), jnp.float32)
            z_m = jnp.zeros((bp, 2, M_HEADS), jnp.float32)
            yp, sc_p, sn_p, sm_p = _mlstm_mixer(hp, *wm, z_c, z_n, z_m)
            ys, _, _, _ = _mlstm_mixer(hs, *wm, state_c[:, o], state_n[:, o], state_m[:, o])
            new_c.append(sc_p)
            new_n.append(sn_p)
            new_m.append(sm_p)
        xp = _residual_norm(xp, yp, mod_p[2], ln_g[l, 0], ln_b[l, 0])
        xs = _residual_norm(xs, ys, mod_s[2], ln_g[l, 0], ln_b[l, 0])
        xp = _residual_norm(xp, _ffn(_modulate(xp, mod_p[3], mod_p[4]), w_ff1[l], w_ff2[l]), mod_p[5], ln_g[l, 1], ln_b[l, 1])
        xs = _residual_norm(xs, _ffn(_modulate(xs, mod_s[3], mod_s[4]), w_ff1[l], w_ff2[l]), mod_s[5], ln_g[l, 1], ln_b[l, 1])
    dt = x_prompt.dtype
    new_cache_k = jnp.stack(new_k, axis=1).astype(dt)
    new_cache_v = jnp.stack(new_v, axis=1).astype(dt)
    new_state_c = jnp.stack(new_c, axis=1).astype(dt)
    new_state_n = jnp.stack(new_n, axis=1).astype(dt)
    new_state_m = jnp.stack(new_m, axis=1).astype(dt)
    return (xp, xs, new_cache_k, new_cache_v, new_state_c, new_state_n, new_state_m)
```

```python
import numpy as np
from contextlib import ExitStack
import concourse.bass as bass
import concourse.mybir as mybir
from concourse.bass_utils import run_bass_kernel_spmd

F32 = mybir.dt.float32
BF16 = mybir.dt.bfloat16
AF = mybir.ActivationFunctionType
ALU = mybir.AluOpType

D = 1024
KC = 8
T = 512
ALPHA = 4.0 ** 0.25
EPS = 1e-6
EPS_RES = EPS / (ALPHA * ALPHA)
NSLOT = 4


class Tk:
    __slots__ = ("w", "r")

    def __init__(self):
        self.w = None
        self.r = []


class B:
    def __init__(self, t, excl=False):
        self.t = t
        self.k = Tk()
        self.excl = excl

    def __getitem__(self, key):
        return self.t[key]


class FW:
    def __init__(self, nc, es, n_dma_sems=48):
        self.nc = nc
        self.eng = {}
        self.sems = []
        for nm in ("pe", "act", "dve", "pool", "sp"):
            h = {"pe": nc.tensor, "act": nc.scalar, "dve": nc.vector, "pool": nc.gpsimd, "sp": nc.sync}[nm]
            s = es.enter_context(nc.semaphore("s_" + nm))
            self.sems.append(s)
            self.eng[nm] = dict(h=h, si=len(self.sems) - 1, cnt=0, seen={})
        self.dsem = []
        for i in range(n_dma_sems):
            s = es.enter_context(nc.semaphore("d%d" % i))
            self.sems.append(s)
            self.dsem.append(dict(si=len(self.sems) - 1, val=0))
        self.dnext = {"hw": 0, "sw": 0}
        self.nsw = 16
        self.ninst = 0

    def _wait(self, e, deps):
        E = self.eng[e]
        best = {}
        for d in deps:
            if d is None:
                continue
            si, v = d
            if best.get(si, 0) < v:
                best[si] = v
        for si, v in best.items():
            if si == E["si"] and e == "pe":
                continue
            if E["seen"].get(si, 0) >= v:
                continue
            E["h"].wait_ge(self.sems[si], v)
            E["seen"][si] = v

    @staticmethod
    def _compact(t):
        if len(t.r) > 16:
            m = {}
            for si, v in t.r:
                if m.get(si, 0) < v:
                    m[si] = v
            t.r = list(m.items())

    def op(self, e, fn, reads=(), writes=(), inc=True):
        E = self.eng[e]
        deps = []
        for b in reads:
            deps.append(b.k.w)
            if b.excl:
                deps.extend(t for t in b.k.r if t[0] != E["si"])
        for b in writes:
            deps.append(b.k.w)
            deps.extend(b.k.r)
        self._wait(e, deps)
        ins = fn(E["h"])
        self.ninst += 1
        tok = (E["si"], E["cnt"] + 1)
        if inc:
            ins.then_inc(self.sems[E["si"]], 1)
            E["cnt"] += 1
        for b in reads:
            b.k.r.append(tok)
            self._compact(b.k)
        for b in writes:
            b.k.w = tok
            b.k.r = []
        return ins

    def dma(self, e, out, in_, reads=(), writes=(), **kw):
        E = self.eng[e]
        if e == "pool":
            ds = self.dsem[self.dnext["sw"]]
            self.dnext["sw"] = (self.dnext["sw"] + 1) % self.nsw
        else:
            ds = self.dsem[self.nsw + self.dnext["hw"]]
            self.dnext["hw"] = (self.dnext["hw"] + 1) % (len(self.dsem) - self.nsw)
        deps = [(ds["si"], ds["val"])] if ds["val"] else []
        for b in reads:
            deps.append(b.k.w)
        for b in writes:
            deps.append(b.k.w)
            deps.extend(b.k.r)
        self._wait(e, deps)
        ins = E["h"].dma_start(out=out, in_=in_, **kw)
        self.ninst += 1
        ds["val"] += 16
        ins.then_inc(self.sems[ds["si"]], 16)
        tok = (ds["si"], ds["val"])
        for b in reads:
            b.k.r.append(tok)
            self._compact(b.k)
        for b in writes:
            b.k.w = tok
            b.k.r = []
        return tok

    def all_tokens(self):
        toks = [(d["si"], d["val"]) for d in self.dsem if d["val"]]
        for nm, E in self.eng.items():
            if E["cnt"]:
                toks.append((E["si"], E["cnt"]))
        return toks

    def barrier(self):
        toks = self.all_tokens()
        for e in ("pe", "act", "dve", "pool", "sp"):
            self._wait(e, toks)


def build_program(stage=99, debug=False):
    nc = bass.Bass("TRN2", target_bir_lowering=False)

    def din(name, shape, dt=F32):
        return nc.dram_tensor(name, list(shape), dt, kind="ExternalInput").ap()

    def dout(name, shape):
        return nc.dram_tensor(name, list(shape), F32, kind="ExternalOutput").ap()

    def dscr(name, shape, dt):
        if debug and dt == F32:
            return B(nc.dram_tensor(name, list(shape), dt, kind="ExternalOutput").ap())
        return B(nc.dram_tensor(name, list(shape), dt).ap())

    xp_d = din("xp", [1024, D])
    xs_d = din("xs", [4096, D])
    ck_d = din("ck", [512, 128])
    cv_d = din("cv", [512, 128])
    stc_d = din("stc", [2, 4, 256, 256])
    stn_d = din("stn", [128, 2, 4, 2, 1])
    stm_d = din("stm", [4, 2])
    cond_d = din("cond", [128, 8, 2])
    wada_d = din("w_ada", [2, D, 6 * D])
    bada_d = din("b_ada", [128, 2, 48])
    lng_d = din("ln_g", [128, 2, 2, 8])
    lnb_d = din("ln_b", [128, 2, 2, 8])
    wia_d = din("w_in_a", [D, 1792])
    woa_d = din("w_out_a", [D, D])
    qkg_d = din("qkg", [128, 2])
    cw_d = din("conv_w", [128, 2, 4, 31])
    cvec_d = din("cvec", [128, 3, 4])
    wf1_d = din("w_ff1", [2, D, 4 * D])
    wf2_d = din("w_ff2", [2, 4 * D, D])
    wim_d = din("w_in_m", [D, 4 * D])
    wg_d = din("w_g", [D, 32])
    bg_d = din("b_g", [4, 2, 4])
    mhg_d = din("mhg", [128, D])
    wom_d = din("w_out_m", [D, D])
    cos_d = din("cosT", [128, 4096])
    sin_d = din("sinT", [128, 4096])
    cst_d = din("cst", [128, 6, 128])
    c4_d = din("c4", [4, 132])
    yp_d = dout("yp", [1024, D])
    ys_d = dout("ys", [2048, D])
    nk_d = dout("nk", [1024, 128])
    nv_d = dout("nv", [1024, 128])
    ncc_d = dout("ncc", [4, 2, 4, 256, 256])
    ncn_d = dout("ncn", [4, 2, 4, 256, 1])
    ncm_d = dout("ncm", [4, 2, 4])
    wia_b = dscr("wia_b", [D, 1792], BF16)
    woa_b = dscr("woa_b", [D, D], BF16)
    wf1_b = [dscr("wf1_b%d" % l, [D, 4 * D], BF16) for l in range(2)]
    wf2_b = [dscr("wf2_b%d" % l, [8, 128, 32 * 128], BF16) for l in range(2)]
    wim_b = dscr("wim_b", [D, 4 * D], BF16)
    wg_b = dscr("wg_b", [D, 32], BF16)
    wom_b = dscr("wom_b", [D, D], BF16)
    x1p_s = [dscr("x1p%d" % j, [128, 8 * T], F32) for j in range(2)]
    x1s_s = [dscr("x1s%d" % j, [128, 8 * T], F32) for j in range(8)]
    hf_s = [dscr("hf%d" % j, [128, 4 * D], F32) for j in range(4)]

    es = ExitStack()
    with es:
        fw = FW(nc, es)

        def sb(name, shape, dt=F32, stack=es):
            return B(stack.enter_context(nc.sbuf_tensor("sb_" + name, list(shape), dt)))

        def op(e, fn, r=(), w=(), inc=True):
            return fw.op(e, fn, expand(r), expand(w), inc)

        def dma(e, out, in_, r=(), w=(), **kw):
            return fw.dma(e, out, in_, expand(r), expand(w), **kw)

        cst = sb("cst", [128, 6, 128])
        ident = cst[:, 0, :]
        bones = cst[:, 1, :]
        onesm = cst[:, 2, :]
        pmt = cst[:, 3, :]
        c4 = sb("c4", [4, 132])
        identb = sb("identb", [128, 128], BF16)
        maskb = sb("maskb", [128, 2, 128], BF16)
        cond = sb("cond", [128, 8, 2])
        scond = sb("scond", [128, 8, 2])
        bada = sb("bada", [128, 2, 48])
        modv = sb("modv", [128, 2, 48, 2])
        lng = sb("lng", [128, 2, 2, 8])
        lnb = sb("lnb", [128, 2, 2, 8])
        dv = sb("dv", [128, 2, 2, 8, 8])
        qkg = sb("qkg", [128, 2])
        cw = sb("cw", [128, 2, 4, 31])
        cvec = sb("cvec", [128, 3, 4])
        bg = sb("bg", [4, 2, 4])
        wgs = sb("wgs", [128, 8, 32], BF16)
        ones1 = sb("ones1", [128, 1], BF16)
        epsc = sb("epsc", [128, 3])
        wsl = [sb("wsl%d" % i, [128, 4096], BF16) for i in range(NSLOT)]
        wnext = [0]
        xres0 = sb("xres", [128, 8, T])

        class Cur:
            def __init__(self, bufs):
                self.i = 0
                self.set(bufs)

            def set(self, bufs):
                self.bufs = bufs
                for b_ in bufs:
                    if not hasattr(b_, "parts"):
                        b_.parts = [B(b_.t[:, kc_, :]) for kc_ in range(8)]

            def __getitem__(self, key):
                return self.bufs[self.i].t[key]

            def p(self, kc_):
                return self.bufs[self.i].parts[kc_]

            def all(self):
                return list(self.bufs[self.i].parts)
        xres = Cur([xres0])

        def expand(lst):
            out = []
            for b_ in lst:
                if isinstance(b_, Cur):
                    out.extend(b_.all())
                elif isinstance(b_, tuple):
                    out.append(b_[0].p(b_[1]))
                else:
                    out.append(b_)
            return out
        big = sb("big", [128, 8192])
        big_bf = big.t[:].bitcast(BF16)
        hb = sb("hb", [128, 8, T + 16], BF16)
        mix = sb("mix", [128, 8, T], BF16)
        mixB = [B(mix.t[:, 0:4, :]), B(mix.t[:, 4:8, :])]
        sq = [sb("sq%d" % i, [128, T]) for i in range(2)]
        mean = sb("mean", [128, T])
        rstd = sb("rstd", [128, T])
        nmr = sb("nmr", [128, T])
        msq = sb("msq", [128, T])
        rt = sq
        psb = [B(es.enter_context(nc.psum_tensor("ps%d" % i, [128, 512], F32)), excl=True) for i in range(7)]
        pstb = B(es.enter_context(nc.psum_tensor("pstb", [128, 1024], BF16)), excl=True)
        pstb_f = pstb.t[:].bitcast(F32)
        pnext = [0]

        ps_n = [5]

        def ps():
            pnext[0] = pnext[0] % ps_n[0]
            p = psb[pnext[0]]
            pnext[0] = (pnext[0] + 1) % ps_n[0]
            return p

        psS_next = [0]

        def ps_S():
            if ps_n[0] == 5:
                return ps()
            p = psb[2 + psS_next[0]]
            psS_next[0] = (psS_next[0] + 1) % 3
            return p

        ponext = [0]

        def ps_acc():
            p = psb[5 + ponext[0]]
            ponext[0] = (ponext[0] + 1) % 2
            return p

        block = es.enter_context(nc.Block())

        dma("sp", cst[:], cst_d, w=[cst])
        dma("sp", c4[:], c4_d, w=[c4])
        dma("sp", cond[:], cond_d, w=[cond])
        dma("sp", bada[:], bada_d, w=[bada])
        dma("sp", lng[:], lng_d, w=[lng])
        dma("sp", lnb[:], lnb_d, w=[lnb])
        dma("sp", qkg[:], qkg_d, w=[qkg])
        dma("sp", cw[:], cw_d, w=[cw])
        dma("sp", cvec[:], cvec_d, w=[cvec])
        dma("sp", bg[:], bg_d, w=[bg])
        op("dve", lambda h: h.tensor_copy(identb[:], ident), [cst], [identb])
        op("dve", lambda h: h.tensor_copy(maskb[:], cst[:, 4:6, :]), [cst], [maskb])
        op("dve", lambda h: h.memset(ones1[:], 1.0), [], [ones1])
        op("dve", lambda h: h.memset(epsc[:, 0:1], EPS_RES), [], [epsc])
        op("dve", lambda h: h.memset(epsc[:, 1:2], EPS), [], [epsc])
        op("dve", lambda h: h.memset(epsc[:, 2:3], 1.0), [], [epsc])

        def conv_rows(dst, src, rows, step=256):
            for r0 in range(0, rows, step):
                dma("pool", dst.t[r0:r0 + step, :], src[r0:r0 + step, :], w=[dst])

        if stage < 0.2:
            conv_rows = lambda *a, **k: None
            conv_w2 = lambda l: None
        conv_rows(wia_b, wia_d, D)
        conv_rows(woa_b, woa_d, D)
        dma("pool", wg_b.t, wg_d, w=[wg_b])
        conv_rows(wf1_b[0], wf1_d[0], D)

        def conv_w2_(l):
            for dc in range(8):
                src = wf2_d[l][:, dc * 128:(dc + 1) * 128].rearrange("(h p) c -> p h c", p=128)
                dst = wf2_b[l].t[dc].rearrange("p (h c) -> p h c", c=128)
                dma("pool", dst, src, w=[wf2_b[l]])

        if stage >= 0.2:
            conv_w2 = conv_w2_
        conv_w2(0)
        conv_rows(wim_b, wim_d, D)
        conv_rows(wom_b, wom_d, D)
        conv_rows(wf1_b[1], wf1_d[1], D)
        conv_w2(1)
        dma("sp", wgs[:], wg_b.t.rearrange("(k p) c -> p k c", p=128), r=[wg_b], w=[wgs])

        op("act", lambda h: h.activation(scond[:], cond[:], AF.Silu), [cond], [scond])
        ada_st = ExitStack()
        wada_sl = [sb("wada%d" % i, [128, 8, 512], F32, ada_st) for i in range(2)]
        mrow = sb("mrow", [2, 6 * D], F32, ada_st)
        for l in range(2 if stage >= 0.3 else 0):
            for pc in range(12):
                wsb = wada_sl[(l * 12 + pc) % 2]
                dma("sp", wsb[:], wada_d[l][:, pc * 512:(pc + 1) * 512].rearrange("(k p) c -> p k c", p=128), w=[wsb])
                pr_ = ps()
                for kc in range(8):
                    op("pe", lambda h, kc=kc, wsb=wsb, pr_=pr_: h.matmul(pr_[0:2, 0:512], scond[:, kc, :], wsb[:, kc, :],
                                                                         start=(kc == 0), stop=(kc == 7)), [wsb, scond], [pr_], inc=(kc == 7))
                op("act", lambda h, pc=pc, pr_=pr_: h.copy(mrow[:, pc * 512:(pc + 1) * 512], pr_[0:2, 0:512]), [pr_], [mrow])
            pm = ps()
            for j in range(48):
                op("pe", lambda h, j=j: h.transpose(pm[:, j * 2:j * 2 + 2], mrow[0:2, j * 128:(j + 1) * 128], ident[0:2, 0:2]), [mrow, cst], [pm], inc=(j == 47))
            op("dve", lambda h, l=l, pm=pm: h.tensor_tensor(
                modv[:, l, :, :], pm[:, 0:96].rearrange("p (j r) -> p j r", r=2),
                bada[:, l, :].unsqueeze(2).to_broadcast([128, 48, 2]), ALU.add), [pm, bada], [modv])
        for l in range(2):
            for r in range(2):
                mv = lambda g: modv[:, l, g * 8:(g + 1) * 8, r]
                d_ = lambda q: dv[:, l, r, q, :]
                op("dve", lambda h: h.tensor_scalar(d_(0), mv(1), 1.0, None, ALU.add), [modv], [dv])
                op("dve", lambda h: h.tensor_copy(d_(1), mv(0)), [modv], [dv])
                op("dve", lambda h: h.tensor_scalar(d_(2), mv(2), 1.0 / ALPHA, None, ALU.mult), [modv], [dv])
                op("dve", lambda h: h.tensor_scalar(d_(3), mv(4), 1.0, None, ALU.add), [modv], [dv])
                op("dve", lambda h: h.tensor_copy(d_(4), mv(3)), [modv], [dv])
                op("dve", lambda h: h.tensor_scalar(d_(5), mv(5), 1.0 / ALPHA, None, ALU.mult), [modv], [dv])
                op("dve", lambda h: h.tensor_tensor(d_(6), lng[:, l, 0, :], d_(3), ALU.mult), [lng, dv], [dv])
                op("dve", lambda h: h.tensor_tensor(d_(7), lnb[:, l, 0, :], d_(3), ALU.mult), [lnb, dv], [dv])
                op("dve", lambda h: h.tensor_tensor(d_(7), d_(7), d_(4), ALU.add), [dv], [dv])

        fw.barrier()
        ada_st.close()

        def wload(src_ap, n):
            s = wsl[wnext[0]]
            wnext[0] = (wnext[0] + 1) % NSLOT
            return s

        def wpiece(scr, c0, ncols):
            s = wsl[wnext[0]]
            wnext[0] = (wnext[0] + 1) % NSLOT
            view = s.t[:, 0:8 * ncols].rearrange("p (k c) -> p k c", c=ncols)
            dma("sp", view, scr.t[:, c0:c0 + ncols].rearrange("(k p) c -> p k c", p=128), r=[scr], w=[s])
            return s, view

        def load_tokens_T(src_rows, nrows, dst_ap_fn, dstB, halo_rows=None):
            stg = big.t[:, 0:4096].rearrange("p (s d) -> p s d", d=D)
            nsub = nrows // 128
            dma("sp", stg[:, 0:nsub, :], src_rows.rearrange("(s p) d -> p s d", p=128), w=[big])
            for kc in range(8):
                p = ps()
                for s in range(nsub):
                    op("pe", lambda h, s=s, kc=kc, p=p: h.transpose(p[:, s * 128:(s + 1) * 128], stg[:, s, kc * 128:(kc + 1) * 128], ident),
                       [big, cst], [p], inc=(s == nsub - 1))
                yield kc, p

        def load_x_tile(src_rows, l, r, keep_x=True, hdst=None):
            ht, hB = (hb, [hb]) if hdst is None else hdst
            for kc, p in load_tokens_T(src_rows, T, None, None):
                if keep_x:
                    op("dve", lambda h, kc=kc, p=p: h.tensor_copy(xres[:, kc, :], p[:, 0:T]), [p], [(xres, kc)])
                op("act", lambda h, kc=kc, p=p: h.activation(ht[:, kc, 0:T], p[:, 0:T], AF.Identity,
                                                             bias=dv[:, l, r, 1, kc:kc + 1], scale=dv[:, l, r, 0, kc:kc + 1]), [p, dv], hB)

        def layernorm_stats(eps):
            p1 = ps()
            for kc in range(8):
                op("pe", lambda h, kc=kc: h.matmul(p1[:, 0:T], onesm, xres[:, kc, :], start=(kc == 0), stop=(kc == 7)), [cst, (xres, kc)], [p1], inc=(kc == 7))
            p2 = ps()
            for kc in range(8):
                s_ = sq[kc % 2]
                op("act", lambda h, kc=kc, s_=s_: h.activation(s_[:], xres[:, kc, :], AF.Square), [(xres, kc)], [s_])
                op("pe", lambda h, kc=kc, s_=s_: h.matmul(p2[:, 0:T], onesm, s_[:], start=(kc == 0), stop=(kc == 7)), [cst, s_], [p2])
            op("act", lambda h: h.copy(mean[:], p1[:, 0:T]), [p1], [mean])
            op("dve", lambda h: h.tensor_tensor(msq[:], mean[:], mean[:], ALU.mult), [mean], [msq])
            op("dve", lambda h: h.tensor_tensor(rstd[:], p2[:, 0:T], msq[:], ALU.subtract), [p2, msq], [rstd])
            op("act", lambda h: h.activation(rstd[:], rstd[:], AF.Sqrt, bias=epsc[:, 0:1], scale=1.0), [rstd, epsc], [rstd])
            op("dve", lambda h: h.reciprocal(rstd[:], rstd[:]), [rstd], [rstd])
            op("dve", lambda h: h.scalar_tensor_tensor(nmr[:], mean[:], -1.0, rstd[:], ALU.mult, ALU.mult), [mean, rstd], [nmr])

        def normalize_xres():
            for kc in range(8):
                op("dve", lambda h, kc=kc: h.tensor_tensor(xres[:, kc, :], xres[:, kc, :], rstd[:], ALU.mult), [(xres, kc), rstd], [(xres, kc)])
                op("dve", lambda h, kc=kc: h.tensor_tensor(xres[:, kc, :], xres[:, kc, :], nmr[:], ALU.add), [(xres, kc), nmr], [(xres, kc)])

        def tail(l, r, wo_scr, wf1, wf2, final_out=None):
            tail_outproj(l, r, wo_scr)
            tail_ln1(l, r)
            for _ in ffn_gen(l, r, wf1, wf2):
                pass
            tail_ln2(l, r)

        def tail_outproj(l, r, wo_scr):
            for half in range(2):
                s, wv = wpiece(wo_scr, half * 512, 512)
                for dj in range(4):
                    dc = half * 4 + dj
                    p = ps()
                    for kc in range(8):
                        op("pe", lambda h, kc=kc, dj=dj, wv=wv, p=p: h.matmul(p[:, 0:T], wv[:, kc, dj * 128:(dj + 1) * 128], mix[:, kc, :],
                                                                          start=(kc == 0), stop=(kc == 7)), [s, mixB[0], mixB[1]], [p], inc=(kc == 7))
                    op("dve", lambda h, dc=dc, p=p: h.scalar_tensor_tensor(xres[:, dc, :], p[:, 0:T], dv[:, l, r, 2, dc:dc + 1], xres[:, dc, :],
                                                                       ALU.mult, ALU.add), [p, dv, (xres, dc)], [(xres, dc)])

        def tail_ln1(l, r):
            layernorm_stats(EPS_RES)
            normalize_xres()
            for kc in range(8):
                op("act", lambda h, kc=kc: h.activation(hb[:, kc, 0:T], xres[:, kc, :], AF.Identity, bias=dv[:, l, r, 7, kc:kc + 1],
                                                        scale=dv[:, l, r, 6, kc:kc + 1]), [(xres, kc), dv], [hb])
                op("pool", lambda h, kc=kc: h.tensor_scalar(xres[:, kc, :], xres[:, kc, :], lng[:, l, 0, kc:kc + 1], lnb[:, l, 0, kc:kc + 1],
                                                            ALU.mult, ALU.add), [(xres, kc), lng, lnb], [(xres, kc)])

        def ffn_gen(l, r, wf1, wf2):
            z = big_bf
            for pc in range(8):
                s, wv = wpiece(wf1, pc * 512, 512)
                for jj in range(4):
                    hc = pc * 4 + jj
                    p = ps()
                    for kc in range(8):
                        op("pe", lambda h, kc=kc, jj=jj, wv=wv, p=p: h.matmul(p[:, 0:T], wv[:, kc, jj * 128:(jj + 1) * 128], hb[:, kc, 0:T],
                                                                          start=(kc == 0), stop=(kc == 7)), [s, hb], [p], inc=(kc == 7))
                        if kc % 2 == 1 and kc < 7:
                            yield
                    rr = rt[hc % 2]
                    op("act", lambda h, p=p, rr=rr: h.activation(rr[:], p[:, 0:T], AF.Relu), [p], [rr])
                    eng = "pool" if hc % 2 == 0 else "dve"
                    op(eng, lambda h, hc=hc, rr=rr: h.tensor_tensor(z[:, hc * T:(hc + 1) * T], rr[:], rr[:], ALU.mult), [rr], [big])
                    yield
            for dc in range(8):
                s = wsl[wnext[0]]
                wnext[0] = (wnext[0] + 1) % NSLOT
                dma("sp", s.t[:], wf2.t[dc], r=[wf2], w=[s])
                p = ps()
                for hc in range(32):
                    op("pe", lambda h, hc=hc, s=s, p=p: h.matmul(p[:, 0:T], s.t[:, hc * 128:(hc + 1) * 128], z[:, hc * T:(hc + 1) * T],
                                                             start=(hc == 0), stop=(hc == 31)), [s, big], [p], inc=(hc == 31))
                    if hc % 2 == 1 and hc < 31:
                        yield
                op("dve", lambda h, dc=dc, p=p: h.scalar_tensor_tensor(xres[:, dc, :], p[:, 0:T], dv[:, l, r, 5, dc:dc + 1], xres[:, dc, :],
                                                                   ALU.mult, ALU.add), [p, dv, (xres, dc)], [(xres, dc)])
                yield

        def tail_ln2(l, r):
            layernorm_stats(EPS_RES)
            normalize_xres()
            for kc in range(8):
                if kc % 2 == 0:
                    op("act", lambda h, kc=kc: h.activation(xres[:, kc, :], xres[:, kc, :], AF.Identity, bias=lnb[:, l, 1, kc:kc + 1],
                                                            scale=lng[:, l, 1, kc:kc + 1]), [(xres, kc), lng, lnb], [(xres, kc)])
                else:
                    op("pool", lambda h, kc=kc: h.tensor_scalar(xres[:, kc, :], xres[:, kc, :], lng[:, l, 1, kc:kc + 1], lnb[:, l, 1, kc:kc + 1],
                                                                ALU.mult, ALU.add), [(xres, kc), lng, lnb], [(xres, kc)])

        def store_tokens(dst_rows, nsub=4):
            stg = big.t[:, 0:4096].rearrange("p (s d) -> p s d", d=D)
            for s_ in range(nsub):
                for half in range(2):
                    p = ps()
                    for kk in range(4):
                        kc = half * 4 + kk
                        op("pe", lambda h, kk=kk, kc=kc, s_=s_, p=p: h.transpose(p[:, kk * 128:(kk + 1) * 128], xres[:, kc, s_ * 128:(s_ + 1) * 128], ident),
                           [(xres, kc), cst], [p], inc=(kk == 3))
                    op("act", lambda h, half=half, s_=s_, p=p: h.copy(stg[:, s_, half * 512:(half + 1) * 512], p[:, 0:512]), [p], [big])
            dma("pool", dst_rows.rearrange("(s p) d -> p s d", p=128), stg[:, 0:nsub, :], r=[big])

        l0 = ExitStack()
        KT = sb("KT", [128, 4608], BF16, l0)
        VA = sb("VA", [128, 36, 2, 128], BF16, l0)
        QT = sb("QT", [128, 4, T], BF16, l0)
        cosb = sb("cosb", [128, T], F32, l0)
        sinb = sb("sinb", [128, T], F32, l0)
        kn = sb("kn", [128, T], F32, l0)
        t1 = sb("t1", [128, T], F32, l0)
        t2 = sb("t2", [128, T], F32, l0)
        U = sb("U", [128, 4, 15 + T + 16], F32, l0)
        acc = sb("acc", [128, 4, T], F32, l0)
        accB = [B(acc.t[:, ch, :]) for ch in range(4)]
        sg = sb("sg", [128, T], F32, l0)
        PT = [sb("PT%d" % i, [128, T], BF16, l0) for i in range(4)]
        rc = sb("rc", [64, T], F32, l0)
        ctm = sb("ctm", [128, 4, 128], F32, l0)
        vst = sb("vst", [128, 4, 128], F32, l0)
        ptn = [0]
        op("pool", lambda h: h.memset(VA[:, :, :, 64:128], 1.0), [], [VA])

        def qk_norm(p, gcol, rope, dst_ap, dstB, tok0=None, extra_out=None):
            op("act", lambda h: h.activation(t1[:], p[:, 0:T], AF.Square), [p], [t1])
            pm = ps()
            op("pe", lambda h: h.matmul(pm[:, 0:T], bones, t1[:], start=True, stop=True), [cst, t1], [pm])
            op("act", lambda h: h.activation(t2[:], pm[:, 0:T], AF.Sqrt, bias=epsc[:, 1:2], scale=1.0), [pm, epsc], [t2])
            op("dve", lambda h: h.reciprocal(t2[:], t2[:]), [t2], [t2])
            if not rope:
                op("dve", lambda h: h.scalar_tensor_tensor(kn[:], p[:, 0:T], qkg[:, gcol:gcol + 1], t2[:], ALU.mult, ALU.mult), [p, qkg, t2], [kn])
                op("act", lambda h: h.copy(dst_ap, kn[:]), [kn], [dstB])
                return
            op("dve", lambda h: h.scalar_tensor_tensor(kn[:], p[:, 0:T], qkg[:, gcol:gcol + 1], t2[:], ALU.mult, ALU.mult), [p, qkg, t2], [kn])
            pr = ps()
            op("pe", lambda h: h.matmul(pr[:, 0:T], pmt, kn[:], start=True, stop=True), [cst, kn], [pr])
            op("pool", lambda h: h.tensor_tensor(t1[:], kn[:], cosb[:], ALU.mult), [kn, cosb], [t1])
            op("dve", lambda h: h.tensor_tensor(t2[:], pr[:, 0:T], sinb[:], ALU.mult), [pr, sinb], [t2])
            op("dve", lambda h: h.tensor_tensor(dst_ap, t1[:], t2[:], ALU.add), [t1, t2], [dstB])

        def proj_fm(wv, s, j, n=T, hsrc=None):
            ht, hB = (hb, [hb]) if hsrc is None else hsrc
            p = ps()
            for kc in range(8):
                op("pe", lambda h, kc=kc: h.matmul(p[:, 0:n], wv[:, kc, j * 128:(j + 1) * 128], ht[:, kc, 0:n],
                                                   start=(kc == 0), stop=(kc == 7)), [s] + hB, [p], inc=(kc == 7))
            return p

        def kv_project(tile_keys0, rope, tok0, want_out=None, hsrc=None):
            ht, hB = (hb, [hb]) if hsrc is None else hsrc
            s, wv = wpiece(wia_b, 512, 256)
            if rope:
                dma("sp", cosb[:], cos_d[:, tok0:tok0 + T], w=[cosb])
                dma("sp", sinb[:], sin_d[:, tok0:tok0 + T], w=[sinb])
            p = proj_fm(wv, s, 0, hsrc=hsrc)
            qk_norm(p, 1, rope, KT[:, tile_keys0:tile_keys0 + T], KT)
            if want_out is not None:
                pt_ = ps()
                for s_ in range(4):
                    op("pe", lambda h, s_=s_: h.transpose(pt_[:, s_ * 128:(s_ + 1) * 128], kn[:, s_ * 128:(s_ + 1) * 128], ident), [kn, cst], [pt_], inc=(s_ == 3))
                op("act", lambda h: h.copy(ctm[:], pt_[:, 0:512].rearrange("p (s c) -> p s c", c=128)), [pt_], [ctm])
                dma("pool", nk_d[want_out:want_out + T, :].rearrange("(s p) c -> p s c", p=128), ctm[:], r=[ctm])
            pv = ps()
            for s_ in range(4):
                for kc in range(8):
                    op("pe", lambda h, s_=s_, kc=kc: h.matmul(pv[:, s_ * 128:(s_ + 1) * 128], ht[:, kc, s_ * 128:(s_ + 1) * 128], wv[:, kc, 128:256],
                                                             start=(kc == 0), stop=(kc == 7)), [s] + hB, [pv], inc=(kc == 7))
            c0 = tile_keys0 // 128
            op("act", lambda h: h.copy(VA[:, c0:c0 + 4, :, 0:64], pv[:, 0:512].rearrange("p (s k d) -> p s k d", k=2, d=64)), [pv], [VA])
            if want_out is not None:
                op("dve", lambda h: h.tensor_copy(vst[:], pv[:, 0:512].rearrange("p (s c) -> p s c", c=128)), [pv], [vst])
                dma("pool", nv_d[want_out:want_out + T, :].rearrange("(s p) c -> p s c", p=128), vst[:], r=[vst])

        def attention(nq, q0, key_chunks, mixcol0, between=None, warm=0):
            for _ in attention_gen(nq, q0, key_chunks, mixcol0, between, warm, 3):
                pass

        def attention_gen(nq, q0, key_chunks, mixcol0, between=None, warm=0, LOOK=3):
            for hd in range(8):
                c = hd % 4
                g = hd // 4
                pr_ = slice(g * 64, (g + 1) * 64)
                po = ps_acc()
                nk_ = len(key_chunks)
                pS_l = [None] * nk_

                def issue_S(i):
                    pS = ps_S()
                    kc = key_chunks[i]
                    op("pe", lambda h, kc=kc, pS=pS: h.matmul(pS[:, 0:nq], KT[pr_, kc * 128:(kc + 1) * 128], QT[pr_, c, q0:q0 + nq], start=True, stop=True),
                       [KT, QT], [pS])
                    pS_l[i] = pS
                for i in range(min(LOOK, nk_)):
                    issue_S(i)
                for i, kc in enumerate(key_chunks):
                    pS = pS_l[i]
                    pt = PT[ptn[0] % 4]
                    ptn[0] += 1
                    op("act", lambda h, pS=pS, pt=pt: h.activation(pt[:, 0:nq], pS[:, 0:nq], AF.Exp, scale=0.125), [pS], [pt])
                    if i + LOOK < nk_:
                        issue_S(i + LOOK)
                    op("pe", lambda h, kc=kc, pt=pt, i=i: h.matmul(po[:, 0:nq], VA[:, kc, g, :], pt[:, 0:nq], start=(i == 0), stop=(i == nk_ - 1)),
                       [VA, pt], [po], inc=(i == nk_ - 1))
                    if warm and i < nk_ - 1:
                        op("pe", lambda h, pt=pt: h.matmul(pstb_f[:, 0:warm], identb[:], pt[:, 0:warm], start=True, stop=True), [identb, pt], [pstb], inc=False)
                    yield
                if between is not None:
                    between(hd)
                op("dve", lambda h: h.reciprocal(rc[:, 0:nq], po[64:128, 0:nq]), [po], [rc])
                op("dve", lambda h: h.tensor_tensor(mix[pr_, c, mixcol0:mixcol0 + nq], po[0:64, 0:nq], rc[:, 0:nq], ALU.mult), [po, rc], [mixB[0]])

        def glu_into_U(wva, sa, wvg, sgt, col0, n, src0):
            for ch in range(4):
                pa = ps()
                pg = ps()
                for kc in range(8):
                    op("pe", lambda h, kc=kc: h.matmul(pa[:, 0:n], wva[:, kc, ch * 128:(ch + 1) * 128], hb[:, kc, src0:src0 + n], start=(kc == 0), stop=(kc == 7)),
                       [sa, hb], [pa], inc=(kc == 7))
                for kc in range(8):
                    op("pe", lambda h, kc=kc: h.matmul(pg[:, 0:n], wvg[:, kc, ch * 128:(ch + 1) * 128], hb[:, kc, src0:src0 + n], start=(kc == 0), stop=(kc == 7)),
                       [sgt, hb], [pg], inc=(kc == 7))
                op("act", lambda h: h.activation(sg[:, 0:n], pg[:, 0:n], AF.Sigmoid), [pg], [sg])
                op("dve", lambda h, ch=ch: h.tensor_tensor(U[:, ch, col0:col0 + n], pa[:, 0:n], sg[:, 0:n], ALU.mult), [pa, sg], [U])

        def conv_taps(which, ucol0, n):
            lst = []
            for ch in range(4):
                lst.append(lambda ch=ch: op("dve", lambda h: h.tensor_scalar(acc[:, ch, 0:n], U[:, ch, ucol0 - 15:ucol0 - 15 + n], cw[:, which, ch, 0:1], cvec[:, 0, ch:ch + 1],
                                                                            ALU.mult, ALU.add), [U, cw, cvec], [accB[ch]]))
                for j in range(1, 31):
                    lst.append(lambda ch=ch, j=j: op("dve", lambda h: h.scalar_tensor_tensor(acc[:, ch, 0:n], U[:, ch, ucol0 - 15 + j:ucol0 - 15 + j + n], cw[:, which, ch, j:j + 1],
                                                                                             acc[:, ch, 0:n], ALU.mult, ALU.add), [U, cw, accB[ch]], [accB[ch]]))
            return lst

        def conv_ln(which, ucol0, n, mixcol0, taps_done=False):
            if not taps_done:
                for f_ in conv_taps(which, ucol0, n):
                    f_()
            p1 = ps()
            for ch in range(4):
                op("pe", lambda h, ch=ch: h.matmul(p1[:, 0:n], onesm, acc[:, ch, 0:n], start=(ch == 0), stop=(ch == 3)), [cst, accB[ch]], [p1], inc=(ch == 3))
            p2 = ps()
            for ch in range(4):
                s_ = sq[ch % 2]
                op("act", lambda h, ch=ch, s_=s_: h.activation(s_[:, 0:n], acc[:, ch, 0:n], AF.Square), [accB[ch]], [s_])
                op("pe", lambda h, ch=ch, s_=s_: h.matmul(p2[:, 0:n], onesm, s_[:, 0:n], start=(ch == 0), stop=(ch == 3)), [cst, s_], [p2])
            op("act", lambda h: h.activation(mean[:, 0:n], p1[:, 0:n], AF.Copy, scale=2.0), [p1], [mean])
            op("dve", lambda h: h.tensor_tensor(msq[:, 0:n], mean[:, 0:n], mean[:, 0:n], ALU.mult), [mean], [msq])
            op("dve", lambda h: h.scalar_tensor_tensor(rstd[:, 0:n], p2[:, 0:n], 2.0, msq[:, 0:n], ALU.mult, ALU.subtract), [p2, msq], [rstd])
            op("act", lambda h: h.activation(rstd[:, 0:n], rstd[:, 0:n], AF.Sqrt, bias=epsc[:, 1:2], scale=1.0), [rstd, epsc], [rstd])
            op("dve", lambda h: h.reciprocal(rstd[:, 0:n], rstd[:, 0:n]), [rstd], [rstd])
            op("dve", lambda h: h.scalar_tensor_tensor(nmr[:, 0:n], mean[:, 0:n], -1.0, rstd[:, 0:n], ALU.mult, ALU.mult), [mean, rstd], [nmr])
            for ch in range(4):
                op("dve", lambda h, ch=ch: h.tensor_tensor(acc[:, ch, 0:n], acc[:, ch, 0:n], rstd[:, 0:n], ALU.mult), [accB[ch], rstd], [accB[ch]])
                op("dve", lambda h, ch=ch: h.tensor_tensor(acc[:, ch, 0:n], acc[:, ch, 0:n], nmr[:, 0:n], ALU.add), [accB[ch], nmr], [accB[ch]])
                op("act", lambda h, ch=ch: h.activation(mix[:, 4 + ch, mixcol0:mixcol0 + n], acc[:, ch, 0:n], AF.Silu, bias=cvec[:, 2, ch:ch + 1],
                                                        scale=cvec[:, 1, ch:ch + 1]), [accB[ch], cvec], [mixB[1]])

        def qk_norm_pair(ps_, dsts, rope=True):
            sets = [(t1, t2, kn), (sq[0], sq[1], msq)]
            R2 = range(2)
            for i in R2:
                a_, b_, k_ = sets[i]
                op("act", lambda h, i=i, a_=a_: h.activation(a_[:], ps_[i][:, 0:T], AF.Square), [ps_[i]], [a_])
            pm_ = [None, None]
            for i in R2:
                a_, b_, k_ = sets[i]
                pm_[i] = ps()
                op("pe", lambda h, i=i, a_=a_: h.matmul(pm_[i][:, 0:T], bones, a_[:], start=True, stop=True), [cst, a_], [pm_[i]])
            for i in R2:
                a_, b_, k_ = sets[i]
                op("act", lambda h, i=i, b_=b_: h.activation(b_[:], pm_[i][:, 0:T], AF.Sqrt, bias=epsc[:, 1:2], scale=1.0), [pm_[i], epsc], [b_])
            for i in R2:
                a_, b_, k_ = sets[i]
                op("dve", lambda h, b_=b_: h.reciprocal(b_[:], b_[:]), [b_], [b_])
            for i in R2:
                a_, b_, k_ = sets[i]
                op("dve", lambda h, i=i, b_=b_, k_=k_: h.scalar_tensor_tensor(k_[:], ps_[i][:, 0:T], qkg[:, 0:1], b_[:], ALU.mult, ALU.mult), [ps_[i], qkg, b_], [k_])
            if not rope:
                for i in R2:
                    a_, b_, k_ = sets[i]
                    op("act", lambda h, i=i, k_=k_: h.copy(dsts[i], k_[:]), [k_], [QT])
                return
            pr_ = [None, None]
            for i in R2:
                a_, b_, k_ = sets[i]
                pr_[i] = ps()
                op("pe", lambda h, i=i, k_=k_: h.matmul(pr_[i][:, 0:T], pmt, k_[:], start=True, stop=True), [cst, k_], [pr_[i]])
            for i in R2:
                a_, b_, k_ = sets[i]
                op("pool", lambda h, a_=a_, k_=k_: h.tensor_tensor(a_[:], k_[:], cosb[:], ALU.mult), [k_, cosb], [a_])
            for i in R2:
                a_, b_, k_ = sets[i]
                op("dve", lambda h, i=i, b_=b_: h.tensor_tensor(b_[:], pr_[i][:, 0:T], sinb[:], ALU.mult), [pr_[i], sinb], [b_])
            for i in R2:
                a_, b_, k_ = sets[i]
                op("dve", lambda h, i=i, a_=a_, b_=b_: h.tensor_tensor(dsts[i], a_[:], b_[:], ALU.add), [a_, b_], [QT])

        def q_project(rope, tok0):
            s, wv = wpiece(wia_b, 0, 512)
            if rope:
                dma("sp", cosb[:], cos_d[:, tok0:tok0 + T], w=[cosb])
                dma("sp", sinb[:], sin_d[:, tok0:tok0 + T], w=[sinb])
            for c2 in range(2):
                pp = [proj_fm(wv, s, 2 * c2), proj_fm(wv, s, 2 * c2 + 1)]
                qk_norm_pair(pp, [QT[:, 2 * c2, :], QT[:, 2 * c2 + 1, :]], rope=rope)

        for j in range(2 if stage >= 0.4 else 0):
            load_x_tile(xp_d[j * T:(j + 1) * T, :], 0, 0)
            if stage >= 0.5:
                kv_project(0, False, 0, want_out=j * T)
            if stage >= 0.6:
                q_project(False, 0)
            if stage >= 0.8:
                sa, wva = wpiece(wia_b, 768, 512)
                sgt, wvg = wpiece(wia_b, 1280, 512)
                for sq_ in range(2):
                    op("pool", lambda h: h.memset(U[:], 0.0), [], [U])
                    glu_into_U(wva, sa, wvg, sgt, 15, 256, sq_ * 256)
                    taps_p = conv_taps(0, 15, 256)

                    def between_p(hd, taps=taps_p):
                        k0, k1 = (len(taps) * hd) // 8, (len(taps) * (hd + 1)) // 8
                        for f_ in taps[k0:k1]:
                            f_()
                    for _ in attention_gen(256, sq_ * 256, [sq_ * 2, sq_ * 2 + 1], sq_ * 256, between=between_p, warm=0, LOOK=2):
                        pass
                    conv_ln(0, 15, 256, sq_ * 256, taps_done=True)
            if stage >= 0.9:
                tail(0, 0, woa_b, wf1_b[0], wf2_b[0])
            dma("pool", x1p_s[j].t.rearrange("p (k t) -> p k t", t=T), xres[:], r=[xres], w=[x1p_s[j]])

        if stage >= 2:
            dma("sp", ctm[:], ck_d.rearrange("(s p) c -> p s c", p=128), w=[ctm])
            pck = ps()
            for s_ in range(4):
                op("pe", lambda h, s_=s_: h.transpose(pck[:, s_ * 128:(s_ + 1) * 128], ctm[:, s_, :], ident), [ctm, cst], [pck], inc=(s_ == 3))
            op("act", lambda h: h.copy(KT[:, 4096:4608], pck[:, 0:512]), [pck], [KT])
            dma("sp", vst[:], cv_d.rearrange("(s p) c -> p s c", p=128), w=[vst])
            op("dve", lambda h: h.tensor_copy(VA[:, 32:36, :, 0:64], vst[:].rearrange("p s (k d) -> p s k d", d=64)), [vst], [VA])
            hbufs = [(hb, [hb]), (mix, [mixB[0], mixB[1]])]
            load_x_tile(xs_d[0:T, :], 0, 1, keep_x=False, hdst=hbufs[0])
            for j in range(8):
                if j < 7:
                    load_x_tile(xs_d[(j + 1) * T:(j + 2) * T, :], 0, 1, keep_x=False, hdst=hbufs[(j + 1) % 2])
                kv_project(j * T, True, j * T, hsrc=hbufs[j % 2])
            xresB = sb("xresB", [128, 8, T], F32, l0)
            xres.set([xres0, xresB])
            op("pool", lambda h: h.memset(U[:], 0.0), [], [U])

            def stage_A(j):
                load_x_tile(xs_d[j * T:(j + 1) * T, :], 0, 1)
                if j < 7:
                    stg = big.t[0:16, 4096:5120]
                    dma("sp", stg, xs_d[(j + 1) * T:(j + 1) * T + 16, :], w=[big])
                    ph = ps()
                    for kc in range(8):
                        op("pe", lambda h, kc=kc: h.transpose(ph[:, kc * 16:(kc + 1) * 16], stg[:, kc * 128:(kc + 1) * 128], ident[0:16, 0:16]), [big, cst], [ph], inc=(kc == 7))
                    for kc in range(8):
                        op("act", lambda h, kc=kc: h.activation(hb[:, kc, T:T + 16], ph[:, kc * 16:(kc + 1) * 16], AF.Identity,
                                                                bias=dv[:, 0, 1, 1, kc:kc + 1], scale=dv[:, 0, 1, 0, kc:kc + 1]), [ph, dv], [hb])
                q_project(True, j * T)
                sa, wva = wpiece(wia_b, 768, 512)
                sgt, wvg = wpiece(wia_b, 1280, 512)
                if j > 0:
                    op("pool", lambda h: h.tensor_copy(U[:, :, 0:15], U[:, :, T:T + 15]), [U], [U])
                glu_into_U(wva, sa, wvg, sgt, 15, T, 0)
                if j < 7:
                    glu_into_U(wva, sa, wvg, sgt, 15 + T, 16, T)
                else:
                    op("pool", lambda h: h.memset(U[:, :, 15 + T:15 + T + 16], 0.0), [], [U])

            def make_attn(warm):
                taps = conv_taps(1, 15, T)

                def between(hd, taps=taps):
                    k0, k1 = (len(taps) * hd) // 8, (len(taps) * (hd + 1)) // 8
                    for f_ in taps[k0:k1]:
                        f_()
                return attention_gen(T, 0, list(range(36)), 0, between=between, warm=warm, LOOK=(3 if warm else 2))

            xres.i = 0
            stage_A(0)
            for _ in make_attn(512):
                pass
            conv_ln(1, 15, T, 0, taps_done=True)
            for j in range(8):
                b_ = j % 2
                xres.i = b_
                tail_outproj(0, 1, woa_b)
                if j < 7:
                    xres.i = 1 - b_
                    stage_A(j + 1)
                    xres.i = b_
                tail_ln1(0, 1)
                fg = ffn_gen(0, 1, wf1_b[0], wf2_b[0])
                if j < 7:
                    ps_n[0] = 2
                    ag = make_attn(0)
                    a_done = f_done = False
                    while not (a_done and f_done):
                        if not a_done:
                            try:
                                next(ag)
                            except StopIteration:
                                a_done = True
                        if not f_done:
                            try:
                                next(fg)
                            except StopIteration:
                                f_done = True
                    ps_n[0] = 5
                else:
                    for _ in fg:
                        pass
                tail_ln2(0, 1)
                dma("pool", x1s_s[j].t.rearrange("p (k t) -> p k t", t=T), xres[:], r=[xres], w=[x1s_s[j]])
                if j < 7:
                    conv_ln(1, 15, T, 0, taps_done=True)
            xres.i = 0
            xres.set([xres0])

        fw.barrier()
        l0.close()

        if stage >= 3:
            l1 = ExitStack()
            mhg = sb("mhg", [128, D], F32, l1)
            dma("sp", mhg[:], mhg_d, w=[mhg])
            qT = sb("qT", [128, 8, T], BF16, l1)
            kT = sb("kT", [128, 8, T], BF16, l1)
            ktm = sb("ktm", [128, 4, D], BF16, l1)
            vaug = sb("vaug", [128, 4, 4, 258], BF16, l1)
            kw = [sb("kw%d" % i, [128, 256], BF16, l1) for i in range(4)]
            smt = [sb("smt%d" % i, [128, 128], BF16, l1) for i in range(4)]
            cst8 = sb("cstate", [128, 2, 4, 2, 258], F32, l1)
            cstB = [[B(cst8.t[:, d_, h_]) for h_ in range(4)] for d_ in range(2)]
            cdb = [sb("cdb%d" % i, [128, 2, 258], BF16, l1) for i in range(4)]
            hacc_t = l1.enter_context(nc.sbuf_tensor("sb_hacc", [128, 4, D], F32))
            hacc = [B(hacc_t[:, c, :]) for c in range(4)]
            haccB = [[B(hacc_t[:, c, h_ * 256:(h_ + 1) * 256]) for h_ in range(4)] for c in range(4)]
            hacc_all = [haccB[c][h_] for c in range(4) for h_ in range(4)]
            gsig = sb("gsig", [128, 4, D], BF16, l1)
            gl = {}
            for gi, k_ in enumerate(("li", "pf", "ab", "lf", "P", "P2", "ws", "cl")):
                gl[k_] = B(big.t[0:4, 4096 + gi * T:4096 + (gi + 1) * T])
                gl[k_].base = 0
            gcar = sb("gcar", [4, 2, 2], F32, l1)
            gdec = sb("gdec", [4, 8], F32, l1)
            gdd = sb("gdd", [4, 4], F32, l1)
            gbd = sb("gbd", [4, 4, 4], F32, l1)
            decb = sb("decb", [128, 16], F32, l1)
            wct = sb("wct", [128, 2, 4, 4], F32, l1)
            dn = sb("dn", [128, 4], F32, l1)
            dnB = [B(dn.t[:, h_:h_ + 1]) for h_ in range(4)]
            ssq = sb("ssq", [128, 16], F32, l1)
            junk = sb("junk", [128, 512], F32, l1)
            mout = sb("mout", [4, 1], F32, l1)
            cn = [0]

            op("pool", lambda h: h.memset(vaug[:, :, :, 256:258], 1.0), [], [vaug])
            for t_ in (gdd, gdec, wct, decb, gbd, dn):
                op("pool", lambda h, t_=t_: h.memset(t_[:], 0.0), [], [t_])
            for t_ in gl.values():
                op("dve", lambda h, t_=t_: h.memset(t_[:], 0.0), [], [t_])

            def tm_proj(wv, s, s_):
                p = ps()
                for kc in range(8):
                    op("pe", lambda h, kc=kc: h.matmul(p[:, 0:512], hb[:, kc, s_ * 128:(s_ + 1) * 128], wv[:, kc, :],
                                                       start=(kc == 0), stop=(kc == 7)), [s, hb], [p], inc=(kc == 7))
                return p

            def in_proj(want_q, want_o):
                if want_q:
                    for half in range(2):
                        s, wv = wpiece(wim_b, half * 512, 512)
                        for jj in range(4):
                            p = proj_fm(wv, s, jj)
                            op("act", lambda h, ch=half * 4 + jj, p=p: h.copy(qT[:, ch, :], p[:, 0:T]), [p], [qT])
                for half in range(2):
                    s, wv = wpiece(wim_b, D + half * 512, 512)
                    if want_q:
                        for jj in range(4):
                            p = proj_fm(wv, s, jj)
                            op("dve", lambda h, ch=half * 4 + jj, p=p: h.tensor_scalar(kT[:, ch, :], p[:, 0:T], 0.0625, None, ALU.mult), [p], [kT])
                    for s_ in range(4):
                        p = tm_proj(wv, s, s_)
                        op("act", lambda h, s_=s_, p=p, half=half: h.activation(ktm[:, s_, half * 512:(half + 1) * 512], p[:, 0:512], AF.Copy, scale=0.0625), [p], [ktm])
                for half in range(2):
                    s, wv = wpiece(wim_b, 2 * D + half * 512, 512)
                    for s_ in range(4):
                        p = tm_proj(wv, s, s_)
                        op("dve", lambda h, s_=s_, p=p, half=half: h.tensor_copy(vaug[:, s_, half * 2:half * 2 + 2, 0:256],
                                                                                 p[:, 0:512].rearrange("p (a b) -> p a b", b=256)), [p], [vaug])
                if want_o:
                    for half in range(2):
                        s, wv = wpiece(wim_b, 3 * D + half * 512, 512)
                        for s_ in range(4):
                            p = tm_proj(wv, s, s_)
                            op("act", lambda h, p=p: h.activation(junk[:, 0:512], p[:, 0:512], AF.Sigmoid), [p], [junk])
                            op("dve", lambda h, s_=s_, half=half: h.tensor_tensor(gsig[:, s_, half * 512:(half + 1) * 512], junk[:, 0:512],
                                                                                  mhg[:, half * 512:(half + 1) * 512], ALU.mult), [junk, mhg], [gsigB[s_]])

            def gates(which, d, asc, t0, n):
                g0 = which * 16 + d * 8
                cs = slice(t0, t0 + n)
                c0 = t0 // 128
                ncn_ = n // 128
                for nm, off in (("li", 0), ("pf", 4)):
                    p = ps()
                    for kc in range(8):
                        op("pe", lambda h, kc=kc, off=off: h.matmul(p[0:4, 0:n], wgs[:, kc, g0 + off:g0 + off + 4], hb[:, kc, cs], start=(kc == 0), stop=(kc == 7)),
                           [wgs, hb], [p], inc=(kc == 7))
                    bi = d * 2 + (0 if nm == "li" else 1)
                    op("act", lambda h, nm=nm, p=p, bi=bi: h.activation(gl[nm][:, cs], p[0:4, 0:n], AF.Identity, bias=bg[:, which, bi:bi + 1], scale=1.0), [p, bg], [gl[nm], big])
                li, pf, ab, lf = gl["li"], gl["pf"], gl["ab"], gl["lf"]
                op("act", lambda h: h.activation(ab[:, cs], pf[:, cs], AF.Abs), [pf], [ab])
                op("act", lambda h: h.activation(ab[:, cs], ab[:, cs], AF.Exp, scale=-1.0), [ab], [ab])
                op("act", lambda h: h.activation(ab[:, cs], ab[:, cs], AF.Ln, bias=epsc[0:4, 2:3], scale=1.0), [ab, epsc], [ab])
                op("dve", lambda h: h.scalar_tensor_tensor(lf[:, cs], pf[:, cs], 0.0, ab[:, cs], ALU.min, ALU.subtract), [pf, ab], [lf])

                def scan(a, b, opx):
                    sh = 1
                    while sh < n:
                        if asc:
                            op("dve", lambda h, a=a, b=b, sh=sh: h.tensor_tensor(b[:, t0 + sh:t0 + n], a[:, t0 + sh:t0 + n], a[:, t0:t0 + n - sh], opx), [a], [b])
                            op("dve", lambda h, a=a, b=b, sh=sh: h.tensor_copy(b[:, t0:t0 + sh], a[:, t0:t0 + sh]), [a], [b])
                        else:
                            op("dve", lambda h, a=a, b=b, sh=sh: h.tensor_tensor(b[:, t0:t0 + n - sh], a[:, t0:t0 + n - sh], a[:, t0 + sh:t0 + n], opx), [a], [b])
                            op("dve", lambda h, a=a, b=b, sh=sh: h.tensor_copy(b[:, t0 + n - sh:t0 + n], a[:, t0 + n - sh:t0 + n]), [a], [b])
                        a, b = b, a
                        sh *= 2
                    return a, b
                Fl, spare = scan(lf, ab, ALU.add)
                op("dve", lambda h: h.tensor_scalar(Fl[:, cs], Fl[:, cs], gcar[:, d, 0:1], None, ALU.add), [Fl, gcar], [Fl])
                r_ = pf
                op("dve", lambda h: h.tensor_tensor(r_[:, cs], li[:, cs], Fl[:, cs], ALU.subtract), [li, Fl], [r_])
                op("dve", lambda h: h.tensor_copy(gl["P"][:, cs], r_[:, cs]), [r_], [gl["P"]])
                Pl, _ = scan(gl["P"], gl["P2"], ALU.max)
                op("dve", lambda h: h.tensor_scalar(Pl[:, cs], Pl[:, cs], gcar[:, d, 1:2], None, ALU.max), [Pl, gcar], [Pl])
                view = lambda x: x[:, cs].rearrange("p (c t) -> p c t", t=128)
                e_off = 127 if asc else 0
                pe_sl = slice(4 + c0, 4 + c0 + ncn_)
                ps_sl = slice(c0, c0 + ncn_)
                op("dve", lambda h: h.tensor_copy(gdec[:, pe_sl], view(Pl)[:, :, e_off]), [Pl], [gdec])
                if asc:
                    if ncn_ > 1:
                        op("dve", lambda h: h.tensor_copy(gdec[:, c0 + 1:c0 + ncn_], gdec[:, 4 + c0:4 + c0 + ncn_ - 1]), [gdec], [gdec])
                    op("dve", lambda h: h.tensor_copy(gdec[:, c0:c0 + 1], gcar[:, d, 1:2]), [gcar, gdec], [gdec])
                else:
                    if ncn_ > 1:
                        op("dve", lambda h: h.tensor_copy(gdec[:, c0:c0 + ncn_ - 1], gdec[:, 4 + c0 + 1:4 + c0 + ncn_]), [gdec], [gdec])
                    op("dve", lambda h: h.tensor_copy(gdec[:, c0 + ncn_ - 1:c0 + ncn_], gcar[:, d, 1:2]), [gcar, gdec], [gdec])
                op("dve", lambda h: h.tensor_tensor(gdd[:, ps_sl], gdec[:, ps_sl], gdec[:, pe_sl], ALU.subtract), [gdec], [gdd])
                op("act", lambda h: h.activation(gdd[:, ps_sl], gdd[:, ps_sl], AF.Exp), [gdd], [gdd])
                pend_b = gdec[:, pe_sl].unsqueeze(2).to_broadcast([4, ncn_, 128])
                e1 = li
                op("dve", lambda h: h.tensor_tensor(view(e1), view(r_), pend_b, ALU.subtract), [r_, gdec], [e1])
                op("act", lambda h: h.activation(gl["ws"][:, cs], e1[:, cs], AF.Exp), [e1], [gl["ws"]])
                op("dve", lambda h: h.scalar_tensor_tensor(view(e1), view(Fl), -1.0, pend_b, ALU.mult, ALU.subtract), [Fl, gdec], [e1])
                op("act", lambda h: h.activation(gl["cl"][:, cs], e1[:, cs], AF.Exp), [e1], [gl["cl"]])
                last = t0 + n - 1 if asc else t0
                op("dve", lambda h: h.tensor_copy(gcar[:, d, 0:1], Fl[:, last:last + 1]), [Fl, gcar], [gcar])
                op("dve", lambda h: h.tensor_copy(gcar[:, d, 1:2], Pl[:, last:last + 1]), [Pl, gcar], [gcar])
                pw = ps()
                for qi, nm in enumerate(("ws", "cl")):
                    for s_ in range(c0, c0 + ncn_):
                        last_ = (qi == 1 and s_ == c0 + ncn_ - 1)
                        op("pe", lambda h, qi=qi, nm=nm, s_=s_: h.transpose(pw[:, (qi * 4 + s_) * 4:(qi * 4 + s_) * 4 + 4], gl[nm][:, s_ * 128:(s_ + 1) * 128], ident[gl[nm].base:gl[nm].base + 4, gl[nm].base:gl[nm].base + 4]),
                           [gl[nm], cst], [pw], inc=last_)
                for qi in range(2):
                    op("act", lambda h, qi=qi: h.copy(wct[:, qi, c0:c0 + ncn_, :], pw[:, (qi * 4 + c0) * 4:(qi * 4 + c0 + ncn_) * 4].rearrange("p (b c) -> p b c", c=4)), [pw], [wct])
                op("dve", lambda h: h.tensor_tensor(gbd[:], gdd[:].unsqueeze(1).to_broadcast([4, 4, 4]), c4[:, 0:4].unsqueeze(2).to_broadcast([4, 4, 4]), ALU.mult),
                   [gdd, c4], [gbd])
                pd = ps()
                op("pe", lambda h: h.matmul(pd[:, 0:16], c4[:, 4:132], gbd[:].rearrange("p a b -> p (a b)"), start=True, stop=True), [c4, gbd], [pd])
                op("act", lambda h: h.copy(decb[:], pd[:, 0:16]), [pd], [decb])

            def mlstm_chunk(d, asc, c, with_out, first_dir):
                mk = maskb[:, 0 if asc else 1, :]
                H = range(4)
                ws_ap = [wct[:, 0, c, hd:hd + 1] for hd in H]
                cl_ap = [wct[:, 1, c, hd:hd + 1] for hd in H]
                dec_ap = [decb[:, hd * 4 + c:hd * 4 + c + 1] for hd in H]
                hsl = [slice(hd * 256, (hd + 1) * 256) for hd in H]
                cs_ = slice(c * 128, (c + 1) * 128)
                for hd in H:
                    op("act", lambda h, hd=hd: h.activation(kw[hd][:], ktm[:, c, hsl[hd]], AF.Copy, scale=ws_ap[hd]), [ktm, wct], [kw[hd]])
                pN = [None] * 4
                if with_out:
                    pS = [None] * 4
                    for hd in H:
                        pS[hd] = ps()
                        for dk in range(2):
                            op("pe", lambda h, dk=dk, hd=hd: h.matmul(pS[hd][:, 0:128], kT[:, 2 * hd + dk, cs_], qT[:, 2 * hd + dk, cs_],
                                                                      start=(dk == 0), stop=(dk == 1)), [kT, qT], [pS[hd]], inc=(dk == 1))
                    for hd in H:
                        op("dve", lambda h, hd=hd: h.scalar_tensor_tensor(smt[hd][:], pS[hd][:, 0:128], ws_ap[hd], mk, ALU.mult, ALU.mult), [pS[hd], wct, maskb], [smt[hd]])
                    for hd in H:
                        op("act", lambda h, hd=hd: h.activation(cdb[hd][:, :, 0:257], cst8[:, d, hd, :, 0:257], AF.Copy, scale=dec_ap[hd]), [cstB[d][hd], decb], [cdb[hd]])
                    for hd in H:
                        pN[hd] = ps()
                        op("pe", lambda h, hd=hd: h.matmul(pN[hd][:, 0:257], smt[hd][:], vaug[:, c, hd, 0:257], start=True, stop=False), [smt[hd], vaug], [pN[hd]], inc=False)
                        for dk in range(2):
                            op("pe", lambda h, dk=dk, hd=hd: h.matmul(pN[hd][:, 0:257], qT[:, 2 * hd + dk, cs_], cdb[hd][:, dk, 0:257], start=False, stop=(dk == 1)),
                               [qT, cdb[hd]], [pN[hd]], inc=(dk == 1))
                    for hd in H:
                        op("act", lambda h, hd=hd: h.activation(dn[:, hd:hd + 1], pN[hd][:, 256:257], AF.Abs), [pN[hd]], [dnB[hd]])
                    for hd in H:
                        op("dve", lambda h, hd=hd: h.tensor_tensor(dn[:, hd:hd + 1], dn[:, hd:hd + 1], cl_ap[hd], ALU.max), [dnB[hd], wct], [dnB[hd]])
                        op("dve", lambda h, hd=hd: h.reciprocal(dn[:, hd:hd + 1], dn[:, hd:hd + 1]), [dnB[hd]], [dnB[hd]])
                    for hd in H:
                        if first_dir:
                            op("act", lambda h, hd=hd: h.activation(hacc_t[:, c, hsl[hd]], pN[hd][:, 0:256], AF.Copy, scale=dn[:, hd:hd + 1]), [pN[hd], dnB[hd]], [haccB[c][hd]])
                        else:
                            op("dve", lambda h, hd=hd: h.scalar_tensor_tensor(hacc_t[:, c, hsl[hd]], pN[hd][:, 0:256], dn[:, hd:hd + 1], hacc_t[:, c, hsl[hd]],
                                                                              ALU.mult, ALU.add), [pN[hd], dnB[hd], haccB[c][hd]], [haccB[c][hd]])
                for hd in H:
                    pC = [ps(), ps()]
                    for dk in range(2):
                        op("pe", lambda h, dk=dk, hd=hd: h.matmul(pC[dk][:, 0:257], kw[hd][:, dk * 128:(dk + 1) * 128], vaug[:, c, hd, 0:257], start=True, stop=True),
                           [kw[hd], vaug], [pC[dk]])
                    for dk in range(2):
                        op("dve", lambda h, dk=dk, hd=hd: h.scalar_tensor_tensor(cst8[:, d, hd, dk, 0:257], cst8[:, d, hd, dk, 0:257], dec_ap[hd], pC[dk][:, 0:257], ALU.mult, ALU.add),
                           [cstB[d][hd], decb, pC[dk]], [cstB[d][hd]])

            ssqB = [B(ssq.t[:, c_ * 4:(c_ + 1) * 4]) for c_ in range(4)]
            gsigB = [B(gsig.t[:, c_, :]) for c_ in range(4)]
            op("pool", lambda h: h.memset(ssq[:], 0.0), [], ssqB)

            def finish_chunk(c):
                for hh in range(2):
                    op("act", lambda h, hh=hh: h.activation(junk[:], hacc_t[:, c, hh * 512:(hh + 1) * 512], AF.Square), haccB[c], [junk])
                    op("dve", lambda h, hh=hh: h.reduce_sum(ssq[:, c * 4 + hh * 2:c * 4 + hh * 2 + 2], junk[:].rearrange("p (a b) -> p a b", b=256), mybir.AxisListType.X), [junk], [ssqB[c]])
                op("act", lambda h: h.activation(ssq[:, c * 4:(c + 1) * 4], ssq[:, c * 4:(c + 1) * 4], AF.Sqrt, bias=epsc[:, 1:2], scale=1.0 / 256.0), [ssqB[c], epsc], [ssqB[c]])
                op("dve", lambda h: h.reciprocal(ssq[:, c * 4:(c + 1) * 4], ssq[:, c * 4:(c + 1) * 4]), [ssqB[c]], [ssqB[c]])
                for hd in range(4):
                    hsl = slice(hd * 256, (hd + 1) * 256)
                    op("dve", lambda h, hd=hd, hsl=hsl: h.scalar_tensor_tensor(gsig[:, c, hsl], hacc_t[:, c, hsl], ssq[:, c * 4 + hd:c * 4 + hd + 1], gsig[:, c, hsl],
                                                                             ALU.mult, ALU.mult), [haccB[c][hd], ssqB[c], gsigB[c]], [gsigB[c]])
                for kc in range(8):
                    op("pe", lambda h, kc=kc: h.transpose(pstb[:, kc * 128:(kc + 1) * 128], gsig[:, c, kc * 128:(kc + 1) * 128], identb[:]), [gsigB[c], identb], [pstb], inc=(kc == 7))
                op("act", lambda h: h.copy(mix[:, :, c * 128:(c + 1) * 128], pstb[:, 0:1024].rearrange("p (k t) -> p k t", t=128)), [pstb], [mixB[0], mixB[1]])

            def load_x1(scr, r):
                dma("sp", xres[:], scr.t.rearrange("p (k t) -> p k t", t=T), r=[scr], w=[xres])
                for kc in range(8):
                    op("act", lambda h, kc=kc: h.activation(hb[:, kc, 0:T], xres[:, kc, :], AF.Identity, bias=dv[:, 1, r, 1, kc:kc + 1], scale=dv[:, 1, r, 0, kc:kc + 1]),
                       [(xres, kc), dv], [hb])

            def init_state(d, zero, src_dir=None):
                if zero:
                    op("pool", lambda h: h.memset(cst8[:, d], 0.0), [], cstB[d])
                    op("pool", lambda h: h.memset(gcar[:, d, :], 0.0), [], [gcar])
                else:
                    for hd in range(4):
                        dma("sp", cst8[:, d, hd, :, 0:256], stc_d[src_dir, hd].rearrange("(k p) v -> p k v", p=128), w=[cstB[d][hd]])
                    dma("sp", cst8[:, d, :, :, 256:257], stn_d[:, src_dir], w=cstB[d], allow_slow_non_contiguous=True)
                    op("pool", lambda h: h.memset(gcar[:, d, 0:1], 0.0), [], [gcar])
                    dma("sp", gcar[:, d, 1:2], stm_d[:, src_dir:src_dir + 1], w=[gcar], allow_slow_non_contiguous=True)

            for j in range(2):
                load_x1(x1p_s[j], 0)
                in_proj(True, True)
                for sq_ in range(2):
                    seq = j * 2 + sq_
                    for d, asc in ((0, True), (1, False)):
                        init_state(d, True)
                        gates(0, d, asc, sq_ * 256, 256)
                        order = [sq_ * 2, sq_ * 2 + 1] if asc else [sq_ * 2 + 1, sq_ * 2]
                        for c in order:
                            mlstm_chunk(d, asc, c, True, d == 0)
                            if d == 1:
                                finish_chunk(c)
                        for hd in range(4):
                            dma("pool", ncc_d[seq, d, hd].rearrange("(k p) v -> p k v", p=128), cst8[:, d, hd, :, 0:256], r=[cstB[d][hd]])
                        dma("pool", ncn_d[seq, d].rearrange("h (k p) o -> p h k o", p=128), cst8[:, d, :, :, 256:257], r=cstB[d], allow_slow_non_contiguous=True)
                        op("dve", lambda h, d=d: h.tensor_tensor(mout[:], gcar[:, d, 0:1], gcar[:, d, 1:2], ALU.add), [gcar], [mout])
                        dma("pool", ncm_d[seq, d].rearrange("(h o) -> h o", o=1), mout[:], r=[mout], allow_slow_non_contiguous=True)
                tail(1, 0, wom_b, wf1_b[1], wf2_b[1])
                store_tokens(yp_d[j * T:(j + 1) * T, :])

            if stage >= 4:
                init_state(0, False, 0)
                for j in range(4):
                    load_x1(x1s_s[j], 1)
                    gates(1, 0, True, 0, T)
                    in_proj(True, False)
                    for c in range(4):
                        mlstm_chunk(0, True, c, True, True)
                    dma("pool", hf_s[j].t.rearrange("p (c f) -> p c f", f=D), hacc_t[:], r=hacc_all, w=[hf_s[j]])
                init_state(1, False, 1)
                for j in range(7, -1, -1):
                    own = j < 4
                    load_x1(x1s_s[j], 1)
                    gates(1, 1, False, 0, T)
                    in_proj(own, own)
                    if own:
                        dma("sp", hacc_t[:], hf_s[j].t.rearrange("p (c f) -> p c f", f=D), r=[hf_s[j]], w=hacc_all)
                    for c in range(3, -1, -1):
                        mlstm_chunk(1, False, c, own, False)
                        if own:
                            finish_chunk(c)
                    if own:
                        tail(1, 1, wom_b, wf1_b[1], wf2_b[1])
                        store_tokens(ys_d[j * T:(j + 1) * T, :])
            l1.close()

        toks = fw.all_tokens()
        for e in ("sp", "pool", "act", "dve", "pe"):
            fw._wait(e, toks)
    return nc


_PROG = {}


def _host_inputs(inp):
    f = lambda a: np.ascontiguousarray(np.asarray(a, dtype=np.float32))
    x_prompt, x_sample = f(inp["x_prompt"]), f(inp["x_sample"])
    cache_k, cache_v = f(inp["cache_k"]), f(inp["cache_v"])
    state_c, state_n, state_m = f(inp["state_c"]), f(inp["state_n"]), f(inp["state_m"])
    c, c_ctx = f(inp["c"]), f(inp["c_ctx"])
    pk = lambda v: np.ascontiguousarray(v.reshape(-1, 128).T)
    b_ada = f(inp["b_ada"])
    bada_l = np.stack([pk(b_ada[l]) for l in range(2)], axis=1)
    ln_g, ln_b = f(inp["ln_g"]), f(inp["ln_b"])
    lng_l = np.stack([np.stack([pk(ln_g[l, w]) for w in range(2)], axis=1) for l in range(2)], axis=1)
    lnb_l = np.stack([np.stack([pk(ln_b[l, w]) for w in range(2)], axis=1) for l in range(2)], axis=1)
    w_in_a = f(inp["w_in_a"])[0]
    w_out_a = f(inp["w_out_a"])[0]
    perm = np.zeros(512, dtype=np.int64)
    for cc in range(4):
        for g in range(2):
            perm[cc * 128 + g * 64:cc * 128 + (g + 1) * 64] = (4 * g + cc) * 64 + np.arange(64)
    wia = w_in_a.copy()
    wia[:, 0:512] = w_in_a[:, perm]
    woa = w_out_a.copy()
    woa[0:512, :] = w_out_a[perm, :]
    qg, kg = f(inp["q_gain"])[0], f(inp["k_gain"])[0]
    qkg = np.stack([np.tile(qg, 2), np.tile(kg, 2)], axis=1)
    conv_w = f(inp["conv_w"])[0]
    cwn = np.ascontiguousarray(conv_w.T.reshape(4, 128, 31).transpose(1, 0, 2))
    cwr = np.ascontiguousarray(cwn[:, :, ::-1])
    cvec = np.stack([pk(f(inp["conv_b"])[0]), pk(f(inp["conv_ln_g"])[0]), pk(f(inp["conv_ln_b"])[0])], axis=1)
    w_in_m = f(inp["w_in_m"])[0]
    wim = np.ascontiguousarray(w_in_m[:, 0:4096])
    wgate = w_in_m[:, 4096:4112]
    b_gate = f(inp["b_gate_m"])[0]
    bgm = b_gate.reshape(4, 4).T
    mhg = np.ascontiguousarray(np.broadcast_to(f(inp["mh_gain"])[0][None, :], (128, 1024)))
    n = 4096
    row = (np.arange(n) // 64).astype(np.float32)
    col = (np.arange(n) % 64).astype(np.float32)
    freqs = (np.float32(10000.0) ** (-np.arange(0, 32, 2, dtype=np.float32) / np.float32(32))).astype(np.float32)
    ang = np.concatenate([row[:, None] * freqs, col[:, None] * freqs], axis=-1).astype(np.float32)
    cos32, sin32 = np.cos(ang), np.sin(ang)
    cosT = np.tile(np.repeat(cos32, 2, axis=1).T, (2, 1)).astype(np.float32)
    sinT = np.tile(np.repeat(sin32, 2, axis=1).T, (2, 1)).astype(np.float32)
    cstm = np.zeros((128, 6, 128), np.float32)
    cstm[:, 0, :] = np.eye(128)
    cstm[0:64, 1, 0:64] = 1.0 / 64
    cstm[64:128, 1, 64:128] = 1.0 / 64
    cstm[:, 2, :] = 1.0 / 1024
    pm = np.zeros((128, 128), np.float32)
    for i in range(64):
        pm[2 * i, 2 * i + 1] = -1.0
        pm[2 * i + 1, 2 * i] = 1.0
    cstm[:, 3, :] = pm.T
    s_ = np.arange(128)[:, None]
    t_ = np.arange(128)[None, :]
    cstm[:, 4, :] = (s_ <= t_)
    cstm[:, 5, :] = (s_ >= t_)
    c4 = np.zeros((4, 132), np.float32)
    c4[:, 0:4] = np.eye(4)
    c4[:, 4:132] = 1.0
    shared = {
        "w_ada": f(inp["w_ada"]), "b_ada": bada_l, "ln_g": lng_l, "ln_b": lnb_l,
        "w_in_a": wia, "w_out_a": woa, "qkg": qkg, "cvec": cvec,
        "w_ff1": f(inp["w_ff1"]), "w_ff2": f(inp["w_ff2"]), "w_in_m": wim, "mhg": mhg,
        "w_out_m": f(inp["w_out_m"])[0], "cst": cstm, "c4": c4,
    }
    maps = []
    for r in range(8):
        s, p = r // 2, r % 2
        m = dict(shared)
        m["xp"] = np.ascontiguousarray(x_prompt[4 * r:4 * r + 4].reshape(1024, 1024))
        xs = x_sample[s]
        m["xs"] = np.ascontiguousarray(xs[::-1]) if p else xs
        m["cosT"] = np.ascontiguousarray(cosT[:, ::-1]) if p else cosT
        m["sinT"] = np.ascontiguousarray(sinT[:, ::-1]) if p else sinT
        m["ck"] = np.ascontiguousarray(cache_k[s, 0].reshape(512, 128))
        m["cv"] = np.ascontiguousarray(cache_v[s, 0].reshape(512, 128))
        dirs = [1, 0] if p else [0, 1]
        m["stc"] = np.ascontiguousarray(state_c[s, 0][dirs])
        stn = state_n[s, 0][dirs]
        m["stn"] = np.ascontiguousarray(stn.reshape(2, 4, 2, 128).transpose(3, 0, 1, 2))[..., None]
        m["stm"] = np.ascontiguousarray(state_m[s, 0][dirs].T)
        m["cond"] = np.ascontiguousarray(np.stack([pk(c_ctx), pk(c[s])], axis=2))
        m["conv_w"] = np.ascontiguousarray(np.stack([cwn, cwr if p else cwn], axis=1))
        gp = wgate
        gs = np.concatenate([wgate[:, 8:16], wgate[:, 0:8]], axis=1) if p else wgate
        m["w_g"] = np.ascontiguousarray(np.concatenate([gp, gs], axis=1))
        bgs = np.concatenate([bgm[:, 2:4], bgm[:, 0:2]], axis=1) if p else bgm
        m["b_g"] = np.ascontiguousarray(np.stack([bgm, bgs], axis=1))
        maps.append(m)
    return maps


def kernel(**inputs):
    if "nc" not in _PROG:
        _PROG["nc"] = build_program()
    nc = _PROG["nc"]
    maps = _host_inputs(inputs)
    res = run_bass_kernel_spmd(nc, maps, core_ids=list(range(8)))
    R = res.results
    y_prompt = np.concatenate([R[r]["yp"].reshape(4, 256, 1024) for r in range(8)], axis=0)
    y_sample = np.zeros((4, 4096, 1024), np.float32)
    for r in range(8):
        s, p = r // 2, r % 2
        ys = R[r]["ys"]
        if p:
            y_sample[s, 2048:4096] = ys[::-1]
        else:
            y_sample[s, 0:2048] = ys
    nk = np.concatenate([R[r]["nk"].reshape(4, 1, 256, 2, 64) for r in range(8)], axis=0)
    nv = np.concatenate([R[r]["nv"].reshape(4, 1, 256, 2, 64) for r in range(8)], axis=0)
    ncc = np.concatenate([R[r]["ncc"].reshape(4, 1, 2, 4, 256, 256) for r in range(8)], axis=0)
    ncn = np.concatenate([R[r]["ncn"].reshape(4, 1, 2, 4, 256) for r in range(8)], axis=0)
    ncm = np.concatenate([R[r]["ncm"].reshape(4, 1, 2, 4) for r in range(8)], axis=0)
    return (y_prompt.astype(np.float32), y_sample, nk.astype(np.float32), nv.astype(np.float32),
            ncc.astype(np.float32), ncn.astype(np.float32), ncm.astype(np.float32))
```

```python
import numpy as np
from contextlib import ExitStack
import concourse.bass as bass
import concourse.mybir as mybir
from concourse.bass_utils import run_bass_kernel_spmd

F32 = mybir.dt.float32
BF16 = mybir.dt.bfloat16
AF = mybir.ActivationFunctionType
ALU = mybir.AluOpType

D = 1024
KC = 8
T = 512
ALPHA = 4.0 ** 0.25
EPS = 1e-6
EPS_RES = EPS / (ALPHA * ALPHA)
NSLOT = 4


class Tk:
    __slots__ = ("w", "r")

    def __init__(self):
        self.w = None
        self.r = []


class B:
    def __init__(self, t, excl=False):
        self.t = t
        self.k = Tk()
        self.excl = excl

    def __getitem__(self, key):
        return self.t[key]


class FW:
    def __init__(self, nc, es, n_dma_sems=48):
        self.nc = nc
        self.eng = {}
        self.sems = []
        for nm in ("pe", "act", "dve", "pool", "sp"):
            h = {"pe": nc.tensor, "act": nc.scalar, "dve": nc.vector, "pool": nc.gpsimd, "sp": nc.sync}[nm]
            s = es.enter_context(nc.semaphore("s_" + nm))
            self.sems.append(s)
            self.eng[nm] = dict(h=h, si=len(self.sems) - 1, cnt=0, seen={})
        self.dsem = []
        for i in range(n_dma_sems):
            s = es.enter_context(nc.semaphore("d%d" % i))
            self.sems.append(s)
            self.dsem.append(dict(si=len(self.sems) - 1, val=0))
        self.dnext = {"hw": 0, "sw": 0}
        self.nsw = 16
        self.ninst = 0

    def _wait(self, e, deps):
        E = self.eng[e]
        best = {}
        for d in deps:
            if d is None:
                continue
            si, v = d
            if best.get(si, 0) < v:
                best[si] = v
        for si, v in best.items():
            if si == E["si"] and e == "pe":
                continue
            if E["seen"].get(si, 0) >= v:
                continue
            E["h"].wait_ge(self.sems[si], v)
            E["seen"][si] = v

    @staticmethod
    def _compact(t):
        if len(t.r) > 16:
            m = {}
            for si, v in t.r:
                if m.get(si, 0) < v:
                    m[si] = v
            t.r = list(m.items())

    def op(self, e, fn, reads=(), writes=(), inc=True):
        E = self.eng[e]
        deps = []
        for b in reads:
            deps.append(b.k.w)
            if b.excl:
                deps.extend(t for t in b.k.r if t[0] != E["si"])
        for b in writes:
            deps.append(b.k.w)
            deps.extend(b.k.r)
        self._wait(e, deps)
        ins = fn(E["h"])
        self.ninst += 1
        tok = (E["si"], E["cnt"] + 1)
        if inc:
            ins.then_inc(self.sems[E["si"]], 1)
            E["cnt"] += 1
        for b in reads:
            b.k.r.append(tok)
            self._compact(b.k)
        for b in writes:
            b.k.w = tok
            b.k.r = []
        return ins

    def dma(self, e, out, in_, reads=(), writes=(), **kw):
        E = self.eng[e]
        if e == "pool":
            ds = self.dsem[self.dnext["sw"]]
            self.dnext["sw"] = (self.dnext["sw"] + 1) % self.nsw
        else:
            ds = self.dsem[self.nsw + self.dnext["hw"]]
            self.dnext["hw"] = (self.dnext["hw"] + 1) % (len(self.dsem) - self.nsw)
        deps = [(ds["si"], ds["val"])] if ds["val"] else []
        for b in reads:
            deps.append(b.k.w)
        for b in writes:
            deps.append(b.k.w)
            deps.extend(b.k.r)
        self._wait(e, deps)
        ins = E["h"].dma_start(out=out, in_=in_, **kw)
        self.ninst += 1
        ds["val"] += 16
        ins.then_inc(self.sems[ds["si"]], 16)
        tok = (ds["si"], ds["val"])
        for b in reads:
            b.k.r.append(tok)
            self._compact(b.k)
        for b in writes:
            b.k.w = tok
            b.k.r = []
        return tok

    def all_tokens(self):
        toks = [(d["si"], d["val"]) for d in self.dsem if d["val"]]
        for nm, E in self.eng.items():
            if E["cnt"]:
                toks.append((E["si"], E["cnt"]))
        return toks

    def barrier(self):
        toks = self.all_tokens()
        for e in ("pe", "act", "dve", "pool", "sp"):
            self._wait(e, toks)


def build_program(stage=99, debug=False):
    nc = bass.Bass("TRN2", target_bir_lowering=False)

    def din(name, shape, dt=F32):
        return nc.dram_tensor(name, list(shape), dt, kind="ExternalInput").ap()

    def dout(name, shape):
        return nc.dram_tensor(name, list(shape), F32, kind="ExternalOutput").ap()

    def dscr(name, shape, dt):
        if debug and dt == F32:
            return B(nc.dram_tensor(name, list(shape), dt, kind="ExternalOutput").ap())
        return B(nc.dram_tensor(name, list(shape), dt).ap())

    xp_d = din("xp", [1024, D])
    xs_d = din("xs", [4096, D])
    ck_d = din("ck", [512, 128])
    cv_d = din("cv", [512, 128])
    stc_d = din("stc", [2, 4, 256, 256])
    stn_d = din("stn", [128, 2, 4, 2, 1])
    stm_d = din("stm", [4, 2])
    cond_d = din("cond", [128, 8, 2])
    wada_d = din("w_ada", [2, D, 6 * D])
    bada_d = din("b_ada", [128, 2, 48])
    lng_d = din("ln_g", [128, 2, 2, 8])
    lnb_d = din("ln_b", [128, 2, 2, 8])
    wia_d = din("w_in_a", [D, 1792])
    woa_d = din("w_out_a", [D, D])
    qkg_d = din("qkg", [128, 2])
    cw_d = din("conv_w", [128, 2, 4, 31])
    cvec_d = din("cvec", [128, 3, 4])
    wf1_d = din("w_ff1", [2, D, 4 * D])
    wf2_d = din("w_ff2", [2, 4 * D, D])
    wim_d = din("w_in_m", [D, 4 * D])
    wg_d = din("w_g", [D, 32])
    bg_d = din("b_g", [4, 2, 4])
    mhg_d = din("mhg", [128, D])
    wom_d = din("w_out_m", [D, D])
    cos_d = din("cosT", [128, 4096])
    sin_d = din("sinT", [128, 4096])
    cst_d = din("cst", [128, 6, 128])
    c4_d = din("c4", [4, 132])
    yp_d = dout("yp", [1024, D])
    ys_d = dout("ys", [2048, D])
    nk_d = dout("nk", [1024, 128])
    nv_d = dout("nv", [1024, 128])
    ncc_d = dout("ncc", [4, 2, 4, 256, 256])
    ncn_d = dout("ncn", [4, 2, 4, 256, 1])
    ncm_d = dout("ncm", [4, 2, 4])
    wia_b = dscr("wia_b", [D, 1792], BF16)
    woa_b = dscr("woa_b", [D, D], BF16)
    wf1_b = [dscr("wf1_b%d" % l, [D, 4 * D], BF16) for l in range(2)]
    wf2_b = [dscr("wf2_b%d" % l, [8, 128, 32 * 128], BF16) for l in range(2)]
    wim_b = dscr("wim_b", [D, 4 * D], BF16)
    wg_b = dscr("wg_b", [D, 32], BF16)
    wom_b = dscr("wom_b", [D, D], BF16)
    x1p_s = [dscr("x1p%d" % j, [128, 8 * T], F32) for j in range(2)]
    x1s_s = [dscr("x1s%d" % j, [128, 8 * T], F32) for j in range(8)]
    hf_s = [dscr("hf%d" % j, [128, 4 * D], F32) for j in range(4)]

    es = ExitStack()
    with es:
        fw = FW(nc, es)

        def sb(name, shape, dt=F32, stack=es):
            return B(stack.enter_context(nc.sbuf_tensor("sb_" + name, list(shape), dt)))

        def op(e, fn, r=(), w=(), inc=True):
            return fw.op(e, fn, expand(r), expand(w), inc)

        def dma(e, out, in_, r=(), w=(), **kw):
            return fw.dma(e, out, in_, expand(r), expand(w), **kw)

        cst = sb("cst", [128, 6, 128])
        ident = cst[:, 0, :]
        bones = cst[:, 1, :]
        onesm = cst[:, 2, :]
        pmt = cst[:, 3, :]
        c4 = sb("c4", [4, 132])
        identb = sb("identb", [128, 128], BF16)
        maskb = sb("maskb", [128, 2, 128], BF16)
        cond = sb("cond", [128, 8, 2])
        scond = sb("scond", [128, 8, 2])
        bada = sb("bada", [128, 2, 48])
        modv = sb("modv", [128, 2, 48, 2])
        lng = sb("lng", [128, 2, 2, 8])
        lnb = sb("lnb", [128, 2, 2, 8])
        dv = sb("dv", [128, 2, 2, 8, 8])
        qkg = sb("qkg", [128, 2])
        cw = sb("cw", [128, 2, 4, 31])
        cvec = sb("cvec", [128, 3, 4])
        bg = sb("bg", [4, 2, 4])
        wgs = sb("wgs", [128, 8, 32], BF16)
        ones1 = sb("ones1", [128, 1], BF16)
        epsc = sb("epsc", [128, 3])
        wsl = [sb("wsl%d" % i, [128, 4096], BF16) for i in range(NSLOT)]
        wnext = [0]
        xres0 = sb("xres", [128, 8, T])

        class Cur:
            def __init__(self, bufs):
                self.i = 0
                self.set(bufs)

            def set(self, bufs):
                self.bufs = bufs
                for b_ in bufs:
                    if not hasattr(b_, "parts"):
                        b_.parts = [B(b_.t[:, kc_, :]) for kc_ in range(8)]

            def __getitem__(self, key):
                return self.bufs[self.i].t[key]

            def p(self, kc_):
                return self.bufs[self.i].parts[kc_]

            def all(self):
                return list(self.bufs[self.i].parts)
        xres = Cur([xres0])

        def expand(lst):
            out = []
            for b_ in lst:
                if isinstance(b_, Cur):
                    out.extend(b_.all())
                elif isinstance(b_, tuple):
                    out.append(b_[0].p(b_[1]))
                else:
                    out.append(b_)
            return out
        big = sb("big", [128, 8192])
        big_bf = big.t[:].bitcast(BF16)
        hb = sb("hb", [128, 8, T + 16], BF16)
        mix = sb("mix", [128, 8, T], BF16)
        mixB = [B(mix.t[:, 0:4, :]), B(mix.t[:, 4:8, :])]
        sq = [sb("sq%d" % i, [128, T]) for i in range(2)]
        mean = sb("mean", [128, T])
        rstd = sb("rstd", [128, T])
        nmr = sb("nmr", [128, T])
        msq = sb("msq", [128, T])
        rt = sq
        psb = [B(es.enter_context(nc.psum_tensor("ps%d" % i, [128, 512], F32)), excl=True) for i in range(7)]
        pstb = B(es.enter_context(nc.psum_tensor("pstb", [128, 1024], BF16)), excl=True)
        pstb_f = pstb.t[:].bitcast(F32)
        pnext = [0]

        ps_n = [5]

        def ps():
            pnext[0] = pnext[0] % ps_n[0]
            p = psb[pnext[0]]
            pnext[0] = (pnext[0] + 1) % ps_n[0]
            return p

        psS_next = [0]

        def ps_S():
            if ps_n[0] == 5:
                return ps()
            p = psb[2 + psS_next[0]]
            psS_next[0] = (psS_next[0] + 1) % 3
            return p

        ponext = [0]

        def ps_acc():
            p = psb[5 + ponext[0]]
            ponext[0] = (ponext[0] + 1) % 2
            return p

        block = es.enter_context(nc.Block())

        dma("sp", cst[:], cst_d, w=[cst])
        dma("sp", c4[:], c4_d, w=[c4])
        dma("sp", cond[:], cond_d, w=[cond])
        dma("sp", bada[:], bada_d, w=[bada])
        dma("sp", lng[:], lng_d, w=[lng])
        dma("sp", lnb[:], lnb_d, w=[lnb])
        dma("sp", qkg[:], qkg_d, w=[qkg])
        dma("sp", cw[:], cw_d, w=[cw])
        dma("sp", cvec[:], cvec_d, w=[cvec])
        dma("sp", bg[:], bg_d, w=[bg])
        op("dve", lambda h: h.tensor_copy(identb[:], ident), [cst], [identb])
        op("dve", lambda h: h.tensor_copy(maskb[:], cst[:, 4:6, :]), [cst], [maskb])
        op("dve", lambda h: h.memset(ones1[:], 1.0), [], [ones1])
        op("dve", lambda h: h.memset(epsc[:, 0:1], EPS_RES), [], [epsc])
        op("dve", lambda h: h.memset(epsc[:, 1:2], EPS), [], [epsc])
        op("dve", lambda h: h.memset(epsc[:, 2:3], 1.0), [], [epsc])

        def conv_rows(dst, src, rows, step=256):
            for r0 in range(0, rows, step):
                dma("pool", dst.t[r0:r0 + step, :], src[r0:r0 + step, :], w=[dst])

        if stage < 0.2:
            conv_rows = lambda *a, **k: None
            conv_w2 = lambda l: None
        conv_rows(wia_b, wia_d, D)
        conv_rows(woa_b, woa_d, D)
        dma("pool", wg_b.t, wg_d, w=[wg_b])
        conv_rows(wf1_b[0], wf1_d[0], D)

        def conv_w2_(l):
            for dc in range(8):
                src = wf2_d[l][:, dc * 128:(dc + 1) * 128].rearrange("(h p) c -> p h c", p=128)
                dst = wf2_b[l].t[dc].rearrange("p (h c) -> p h c", c=128)
                dma("pool", dst, src, w=[wf2_b[l]])

        if stage >= 0.2:
            conv_w2 = conv_w2_
        conv_w2(0)
        conv_rows(wim_b, wim_d, D)
        conv_rows(wom_b, wom_d, D)
        conv_rows(wf1_b[1], wf1_d[1], D)
        conv_w2(1)
        dma("sp", wgs[:], wg_b.t.rearrange("(k p) c -> p k c", p=128), r=[wg_b], w=[wgs])

        op("act", lambda h: h.activation(scond[:], cond[:], AF.Silu), [cond], [scond])
        ada_st = ExitStack()
        wada_sl = [sb("wada%d" % i, [128, 8, 512], F32, ada_st) for i in range(2)]
        mrow = sb("mrow", [2, 6 * D], F32, ada_st)
        for l in range(2 if stage >= 0.3 else 0):
            for pc in range(12):
                wsb = wada_sl[(l * 12 + pc) % 2]
                dma("sp", wsb[:], wada_d[l][:, pc * 512:(pc + 1) * 512].rearrange("(k p) c -> p k c", p=128), w=[wsb])
                pr_ = ps()
                for kc in range(8):
                    op("pe", lambda h, kc=kc, wsb=wsb, pr_=pr_: h.matmul(pr_[0:2, 0:512], scond[:, kc, :], wsb[:, kc, :],
                                                                         start=(kc == 0), stop=(kc == 7)), [wsb, scond], [pr_], inc=(kc == 7))
                op("act", lambda h, pc=pc, pr_=pr_: h.copy(mrow[:, pc * 512:(pc + 1) * 512], pr_[0:2, 0:512]), [pr_], [mrow])
            pm = ps()
            for j in range(48):
                op("pe", lambda h, j=j: h.transpose(pm[:, j * 2:j * 2 + 2], mrow[0:2, j * 128:(j + 1) * 128], ident[0:2, 0:2]), [mrow, cst], [pm], inc=(j == 47))
            op("dve", lambda h, l=l, pm=pm: h.tensor_tensor(
                modv[:, l, :, :], pm[:, 0:96].rearrange("p (j r) -> p j r", r=2),
                bada[:, l, :].unsqueeze(2).to_broadcast([128, 48, 2]), ALU.add), [pm, bada], [modv])
        for l in range(2):
            for r in range(2):
                mv = lambda g: modv[:, l, g * 8:(g + 1) * 8, r]
                d_ = lambda q: dv[:, l, r, q, :]
                op("dve", lambda h: h.tensor_scalar(d_(0), mv(1), 1.0, None, ALU.add), [modv], [dv])
                op("dve", lambda h: h.tensor_copy(d_(1), mv(0)), [modv], [dv])
                op("dve", lambda h: h.tensor_scalar(d_(2), mv(2), 1.0 / ALPHA, None, ALU.mult), [modv], [dv])
                op("dve", lambda h: h.tensor_scalar(d_(3), mv(4), 1.0, None, ALU.add), [modv], [dv])
                op("dve", lambda h: h.tensor_copy(d_(4), mv(3)), [modv], [dv])
                op("dve", lambda h: h.tensor_scalar(d_(5), mv(5), 1.0 / ALPHA, None, ALU.mult), [modv], [dv])
                op("dve", lambda h: h.tensor_tensor(d_(6), lng[:, l, 0, :], d_(3), ALU.mult), [lng, dv], [dv])
                op("dve", lambda h: h.tensor_tensor(d_(7), lnb[:, l, 0, :], d_(3), ALU.mult), [lnb, dv], [dv])
                op("dve", lambda h: h.tensor_tensor(d_(7), d_(7), d_(4), ALU.add), [dv], [dv])

        fw.barrier()
        ada_st.close()

        def wload(src_ap, n):
            s = wsl[wnext[0]]
            wnext[0] = (wnext[0] + 1) % NSLOT
            return s

        def wpiece(scr, c0, ncols):
            s = wsl[wnext[0]]
            wnext[0] = (wnext[0] + 1) % NSLOT
            view = s.t[:, 0:8 * ncols].rearrange("p (k c) -> p k c", c=ncols)
            dma("sp", view, scr.t[:, c0:c0 + ncols].rearrange("(k p) c -> p k c", p=128), r=[scr], w=[s])
            return s, view

        def load_tokens_T(src_rows, nrows, dst_ap_fn, dstB, halo_rows=None):
            stg = big.t[:, 0:4096].rearrange("p (s d) -> p s d", d=D)
            nsub = nrows // 128
            dma("sp", stg[:, 0:nsub, :], src_rows.rearrange("(s p) d -> p s d", p=128), w=[big])
            for kc in range(8):
                p = ps()
                for s in range(nsub):
                    op("pe", lambda h, s=s, kc=kc, p=p: h.transpose(p[:, s * 128:(s + 1) * 128], stg[:, s, kc * 128:(kc + 1) * 128], ident),
                       [big, cst], [p], inc=(s == nsub - 1))
                yield kc, p

        def load_x_tile(src_rows, l, r, keep_x=True, hdst=None):
            ht, hB = (hb, [hb]) if hdst is None else hdst
            for kc, p in load_tokens_T(src_rows, T, None, None):
                if keep_x:
                    op("dve", lambda h, kc=kc, p=p: h.tensor_copy(xres[:, kc, :], p[:, 0:T]), [p], [(xres, kc)])
                op("act", lambda h, kc=kc, p=p: h.activation(ht[:, kc, 0:T], p[:, 0:T], AF.Identity,
                                                             bias=dv[:, l, r, 1, kc:kc + 1], scale=dv[:, l, r, 0, kc:kc + 1]), [p, dv], hB)

        def layernorm_stats(eps):
            p1 = ps()
            for kc in range(8):
                op("pe", lambda h, kc=kc: h.matmul(p1[:, 0:T], onesm, xres[:, kc, :], start=(kc == 0), stop=(kc == 7)), [cst, (xres, kc)], [p1], inc=(kc == 7))
            p2 = ps()
            for kc in range(8):
                s_ = sq[kc % 2]
                op("act", lambda h, kc=kc, s_=s_: h.activation(s_[:], xres[:, kc, :], AF.Square), [(xres, kc)], [s_])
                op("pe", lambda h, kc=kc, s_=s_: h.matmul(p2[:, 0:T], onesm, s_[:], start=(kc == 0), stop=(kc == 7)), [cst, s_], [p2])
            op("act", lambda h: h.copy(mean[:], p1[:, 0:T]), [p1], [mean])
            op("dve", lambda h: h.tensor_tensor(msq[:], mean[:], mean[:], ALU.mult), [mean], [msq])
            op("dve", lambda h: h.tensor_tensor(rstd[:], p2[:, 0:T], msq[:], ALU.subtract), [p2, msq], [rstd])
            op("act", lambda h: h.activation(rstd[:], rstd[:], AF.Ln, bias=epsc[:, 0:1], scale=1.0), [rstd, epsc], [rstd])
            op("act", lambda h: h.activation(rstd[:], rstd[:], AF.Exp, scale=-0.5), [rstd], [rstd])
            op("dve", lambda h: h.scalar_tensor_tensor(nmr[:], mean[:], -1.0, rstd[:], ALU.mult, ALU.mult), [mean, rstd], [nmr])

        def normalize_xres():
            for kc in range(8):
                op("dve", lambda h, kc=kc: h.tensor_tensor(xres[:, kc, :], xres[:, kc, :], rstd[:], ALU.mult), [(xres, kc), rstd], [(xres, kc)])
                op("dve", lambda h, kc=kc: h.tensor_tensor(xres[:, kc, :], xres[:, kc, :], nmr[:], ALU.add), [(xres, kc), nmr], [(xres, kc)])

        def tail(l, r, wo_scr, wf1, wf2, final_out=None):
            tail_outproj(l, r, wo_scr)
            tail_ln1(l, r)
            for _ in ffn_gen(l, r, wf1, wf2):
                pass
            tail_ln2(l, r)

        def tail_outproj(l, r, wo_scr):
            for half in range(2):
                s, wv = wpiece(wo_scr, half * 512, 512)
                for dj in range(4):
                    dc = half * 4 + dj
                    p = ps()
                    for kc in range(8):
                        op("pe", lambda h, kc=kc, dj=dj, wv=wv, p=p: h.matmul(p[:, 0:T], wv[:, kc, dj * 128:(dj + 1) * 128], mix[:, kc, :],
                                                                          start=(kc == 0), stop=(kc == 7)), [s, mixB[0], mixB[1]], [p], inc=(kc == 7))
                    op("dve", lambda h, dc=dc, p=p: h.scalar_tensor_tensor(xres[:, dc, :], p[:, 0:T], dv[:, l, r, 2, dc:dc + 1], xres[:, dc, :],
                                                                       ALU.mult, ALU.add), [p, dv, (xres, dc)], [(xres, dc)])

        def tail_ln1(l, r):
            layernorm_stats(EPS_RES)
            normalize_xres()
            for kc in range(8):
                op("act", lambda h, kc=kc: h.activation(hb[:, kc, 0:T], xres[:, kc, :], AF.Identity, bias=dv[:, l, r, 7, kc:kc + 1],
                                                        scale=dv[:, l, r, 6, kc:kc + 1]), [(xres, kc), dv], [hb])
                op("pool", lambda h, kc=kc: h.tensor_scalar(xres[:, kc, :], xres[:, kc, :], lng[:, l, 0, kc:kc + 1], lnb[:, l, 0, kc:kc + 1],
                                                            ALU.mult, ALU.add), [(xres, kc), lng, lnb], [(xres, kc)])

        def ffn_gen(l, r, wf1, wf2):
            z = big_bf
            for pc in range(8):
                s, wv = wpiece(wf1, pc * 512, 512)
                for jj in range(4):
                    hc = pc * 4 + jj
                    p = ps()
                    for kc in range(8):
                        op("pe", lambda h, kc=kc, jj=jj, wv=wv, p=p: h.matmul(p[:, 0:T], wv[:, kc, jj * 128:(jj + 1) * 128], hb[:, kc, 0:T],
                                                                          start=(kc == 0), stop=(kc == 7)), [s, hb], [p], inc=(kc == 7))
                        if kc % 2 == 1 and kc < 7:
                            yield
                    rr = rt[hc % 2]
                    op("act", lambda h, p=p, rr=rr: h.activation(rr[:], p[:, 0:T], AF.Relu), [p], [rr])
                    eng = "pool" if hc % 2 == 0 else "dve"
                    op(eng, lambda h, hc=hc, rr=rr: h.tensor_tensor(z[:, hc * T:(hc + 1) * T], rr[:], rr[:], ALU.mult), [rr], [big])
                    yield
            for dc in range(8):
                s = wsl[wnext[0]]
                wnext[0] = (wnext[0] + 1) % NSLOT
                dma("sp", s.t[:], wf2.t[dc], r=[wf2], w=[s])
                p = ps()
                for hc in range(32):
                    op("pe", lambda h, hc=hc, s=s, p=p: h.matmul(p[:, 0:T], s.t[:, hc * 128:(hc + 1) * 128], z[:, hc * T:(hc + 1) * T],
                                                             start=(hc == 0), stop=(hc == 31)), [s, big], [p], inc=(hc == 31))
                    if hc % 2 == 1 and hc < 31:
                        yield
                op("dve", lambda h, dc=dc, p=p: h.scalar_tensor_tensor(xres[:, dc, :], p[:, 0:T], dv[:, l, r, 5, dc:dc + 1], xres[:, dc, :],
                                                                   ALU.mult, ALU.add), [p, dv, (xres, dc)], [(xres, dc)])
                yield

        def tail_ln2(l, r):
            layernorm_stats(EPS_RES)
            normalize_xres()
            for kc in range(8):
                if kc % 2 == 0:
                    op("act", lambda h, kc=kc: h.activation(xres[:, kc, :], xres[:, kc, :], AF.Identity, bias=lnb[:, l, 1, kc:kc + 1],
                                                            scale=lng[:, l, 1, kc:kc + 1]), [(xres, kc), lng, lnb], [(xres, kc)])
                else:
                    op("pool", lambda h, kc=kc: h.tensor_scalar(xres[:, kc, :], xres[:, kc, :], lng[:, l, 1, kc:kc + 1], lnb[:, l, 1, kc:kc + 1],
                                                                ALU.mult, ALU.add), [(xres, kc), lng, lnb], [(xres, kc)])

        def store_tokens(dst_rows, nsub=4):
            stg = big.t[:, 0:4096].rearrange("p (s d) -> p s d", d=D)
            for s_ in range(nsub):
                for half in range(2):
                    p = ps()
                    for kk in range(4):
                        kc = half * 4 + kk
                        op("pe", lambda h, kk=kk, kc=kc, s_=s_, p=p: h.transpose(p[:, kk * 128:(kk + 1) * 128], xres[:, kc, s_ * 128:(s_ + 1) * 128], ident),
                           [(xres, kc), cst], [p], inc=(kk == 3))
                    op("act", lambda h, half=half, s_=s_, p=p: h.copy(stg[:, s_, half * 512:(half + 1) * 512], p[:, 0:512]), [p], [big])
            dma("pool", dst_rows.rearrange("(s p) d -> p s d", p=128), stg[:, 0:nsub, :], r=[big])

        l0 = ExitStack()
        KT = sb("KT", [128, 4608], BF16, l0)
        VA = sb("VA", [128, 36, 2, 128], BF16, l0)
        QT = sb("QT", [128, 4, T], BF16, l0)
        cosb = sb("cosb", [128, T], F32, l0)
        sinb = sb("sinb", [128, T], F32, l0)
        kn = sb("kn", [128, T], F32, l0)
        t1 = sb("t1", [128, T], F32, l0)
        t2 = sb("t2", [128, T], F32, l0)
        U = sb("U", [128, 4, 15 + T + 16], F32, l0)
        acc = sb("acc", [128, 4, T], F32, l0)
        accB = [B(acc.t[:, ch, :]) for ch in range(4)]
        sg = sb("sg", [128, T], F32, l0)
        PT = [sb("PT%d" % i, [128, T], BF16, l0) for i in range(4)]
        rc = sb("rc", [64, T], F32, l0)
        ctm = sb("ctm", [128, 4, 128], F32, l0)
        vst = sb("vst", [128, 4, 128], F32, l0)
        ptn = [0]
        op("pool", lambda h: h.memset(VA[:, :, :, 64:128], 1.0), [], [VA])

        def qk_norm(p, gcol, rope, dst_ap, dstB, tok0=None, extra_out=None):
            op("act", lambda h: h.activation(t1[:], p[:, 0:T], AF.Square), [p], [t1])
            pm = ps()
            op("pe", lambda h: h.matmul(pm[:, 0:T], bones, t1[:], start=True, stop=True), [cst, t1], [pm])
            op("act", lambda h: h.activation(t2[:], pm[:, 0:T], AF.Ln, bias=epsc[:, 1:2], scale=1.0), [pm, epsc], [t2])
            op("act", lambda h: h.activation(t2[:], t2[:], AF.Exp, scale=-0.5), [t2], [t2])
            if not rope:
                op("dve", lambda h: h.scalar_tensor_tensor(kn[:], p[:, 0:T], qkg[:, gcol:gcol + 1], t2[:], ALU.mult, ALU.mult), [p, qkg, t2], [kn])
                op("act", lambda h: h.copy(dst_ap, kn[:]), [kn], [dstB])
                return
            op("dve", lambda h: h.scalar_tensor_tensor(kn[:], p[:, 0:T], qkg[:, gcol:gcol + 1], t2[:], ALU.mult, ALU.mult), [p, qkg, t2], [kn])
            pr = ps()
            op("pe", lambda h: h.matmul(pr[:, 0:T], pmt, kn[:], start=True, stop=True), [cst, kn], [pr])
            op("pool", lambda h: h.tensor_tensor(t1[:], kn[:], cosb[:], ALU.mult), [kn, cosb], [t1])
            op("dve", lambda h: h.tensor_tensor(t2[:], pr[:, 0:T], sinb[:], ALU.mult), [pr, sinb], [t2])
            op("dve", lambda h: h.tensor_tensor(dst_ap, t1[:], t2[:], ALU.add), [t1, t2], [dstB])

        def proj_fm(wv, s, j, n=T, hsrc=None):
            ht, hB = (hb, [hb]) if hsrc is None else hsrc
            p = ps()
            for kc in range(8):
                op("pe", lambda h, kc=kc: h.matmul(p[:, 0:n], wv[:, kc, j * 128:(j + 1) * 128], ht[:, kc, 0:n],
                                                   start=(kc == 0), stop=(kc == 7)), [s] + hB, [p], inc=(kc == 7))
            return p

        def kv_project(tile_keys0, rope, tok0, want_out=None, hsrc=None):
            ht, hB = (hb, [hb]) if hsrc is None else hsrc
            s, wv = wpiece(wia_b, 512, 256)
            if rope:
                dma("sp", cosb[:], cos_d[:, tok0:tok0 + T], w=[cosb])
                dma("sp", sinb[:], sin_d[:, tok0:tok0 + T], w=[sinb])
            p = proj_fm(wv, s, 0, hsrc=hsrc)
            qk_norm(p, 1, rope, KT[:, tile_keys0:tile_keys0 + T], KT)
            if want_out is not None:
                pt_ = ps()
                for s_ in range(4):
                    op("pe", lambda h, s_=s_: h.transpose(pt_[:, s_ * 128:(s_ + 1) * 128], kn[:, s_ * 128:(s_ + 1) * 128], ident), [kn, cst], [pt_], inc=(s_ == 3))
                op("act", lambda h: h.copy(ctm[:], pt_[:, 0:512].rearrange("p (s c) -> p s c", c=128)), [pt_], [ctm])
                dma("pool", nk_d[want_out:want_out + T, :].rearrange("(s p) c -> p s c", p=128), ctm[:], r=[ctm])
            pv = ps()
            for s_ in range(4):
                for kc in range(8):
                    op("pe", lambda h, s_=s_, kc=kc: h.matmul(pv[:, s_ * 128:(s_ + 1) * 128], ht[:, kc, s_ * 128:(s_ + 1) * 128], wv[:, kc, 128:256],
                                                             start=(kc == 0), stop=(kc == 7)), [s] + hB, [pv], inc=(kc == 7))
            c0 = tile_keys0 // 128
            op("act", lambda h: h.copy(VA[:, c0:c0 + 4, :, 0:64], pv[:, 0:512].rearrange("p (s k d) -> p s k d", k=2, d=64)), [pv], [VA])
            if want_out is not None:
                op("dve", lambda h: h.tensor_copy(vst[:], pv[:, 0:512].rearrange("p (s c) -> p s c", c=128)), [pv], [vst])
                dma("pool", nv_d[want_out:want_out + T, :].rearrange("(s p) c -> p s c", p=128), vst[:], r=[vst])

        def attention(nq, q0, key_chunks, mixcol0, between=None, warm=0):
            for _ in attention_gen(nq, q0, key_chunks, mixcol0, between, warm, 3):
                pass

        def attention_gen(nq, q0, key_chunks, mixcol0, between=None, warm=0, LOOK=3):
            for hd in range(8):
                c = hd % 4
                g = hd // 4
                pr_ = slice(g * 64, (g + 1) * 64)
                po = ps_acc()
                nk_ = len(key_chunks)
                pS_l = [None] * nk_

                def issue_S(i):
                    pS = ps_S()
                    kc = key_chunks[i]
                    op("pe", lambda h, kc=kc, pS=pS: h.matmul(pS[:, 0:nq], KT[pr_, kc * 128:(kc + 1) * 128], QT[pr_, c, q0:q0 + nq], start=True, stop=True),
                       [KT, QT], [pS])
                    pS_l[i] = pS
                for i in range(min(LOOK, nk_)):
                    issue_S(i)
                for i, kc in enumerate(key_chunks):
                    pS = pS_l[i]
                    pt = PT[ptn[0] % 4]
                    ptn[0] += 1
                    op("act", lambda h, pS=pS, pt=pt: h.activation(pt[:, 0:nq], pS[:, 0:nq], AF.Exp, scale=0.125), [pS], [pt])
                    if i + LOOK < nk_:
                        issue_S(i + LOOK)
                    op("pe", lambda h, kc=kc, pt=pt, i=i: h.matmul(po[:, 0:nq], VA[:, kc, g, :], pt[:, 0:nq], start=(i == 0), stop=(i == nk_ - 1)),
                       [VA, pt], [po], inc=(i == nk_ - 1))
                    if warm and i < nk_ - 1:
                        op("pe", lambda h, pt=pt: h.matmul(pstb_f[:, 0:warm], identb[:], pt[:, 0:warm], start=True, stop=True), [identb, pt], [pstb], inc=False)
                    yield
                if between is not None:
                    between(hd)
                op("dve", lambda h: h.reciprocal(rc[:, 0:nq], po[64:128, 0:nq]), [po], [rc])
                op("dve", lambda h: h.tensor_tensor(mix[pr_, c, mixcol0:mixcol0 + nq], po[0:64, 0:nq], rc[:, 0:nq], ALU.mult), [po, rc], [mixB[0]])

        def glu_into_U(wva, sa, wvg, sgt, col0, n, src0):
            for ch in range(4):
                pa = ps()
                pg = ps()
                for kc in range(8):
                    op("pe", lambda h, kc=kc: h.matmul(pa[:, 0:n], wva[:, kc, ch * 128:(ch + 1) * 128], hb[:, kc, src0:src0 + n], start=(kc == 0), stop=(kc == 7)),
                       [sa, hb], [pa], inc=(kc == 7))
                for kc in range(8):
                    op("pe", lambda h, kc=kc: h.matmul(pg[:, 0:n], wvg[:, kc, ch * 128:(ch + 1) * 128], hb[:, kc, src0:src0 + n], start=(kc == 0), stop=(kc == 7)),
                       [sgt, hb], [pg], inc=(kc == 7))
                op("act", lambda h: h.activation(sg[:, 0:n], pg[:, 0:n], AF.Sigmoid), [pg], [sg])
                op("dve", lambda h, ch=ch: h.tensor_tensor(U[:, ch, col0:col0 + n], pa[:, 0:n], sg[:, 0:n], ALU.mult), [pa, sg], [U])

        def conv_taps(which, ucol0, n):
            lst = []
            for ch in range(4):
                lst.append(lambda ch=ch: op("dve", lambda h: h.tensor_scalar(acc[:, ch, 0:n], U[:, ch, ucol0 - 15:ucol0 - 15 + n], cw[:, which, ch, 0:1], cvec[:, 0, ch:ch + 1],
                                                                            ALU.mult, ALU.add), [U, cw, cvec], [accB[ch]]))
                for j in range(1, 31):
                    lst.append(lambda ch=ch, j=j: op("dve", lambda h: h.scalar_tensor_tensor(acc[:, ch, 0:n], U[:, ch, ucol0 - 15 + j:ucol0 - 15 + j + n], cw[:, which, ch, j:j + 1],
                                                                                             acc[:, ch, 0:n], ALU.mult, ALU.add), [U, cw, accB[ch]], [accB[ch]]))
            return lst

        def conv_ln(which, ucol0, n, mixcol0, taps_done=False):
            if not taps_done:
                for f_ in conv_taps(which, ucol0, n):
                    f_()
            p1 = ps()
            for ch in range(4):
                op("pe", lambda h, ch=ch: h.matmul(p1[:, 0:n], onesm, acc[:, ch, 0:n], start=(ch == 0), stop=(ch == 3)), [cst, accB[ch]], [p1], inc=(ch == 3))
            p2 = ps()
            for ch in range(4):
                s_ = sq[ch % 2]
                op("act", lambda h, ch=ch, s_=s_: h.activation(s_[:, 0:n], acc[:, ch, 0:n], AF.Square), [accB[ch]], [s_])
                op("pe", lambda h, ch=ch, s_=s_: h.matmul(p2[:, 0:n], onesm, s_[:, 0:n], start=(ch == 0), stop=(ch == 3)), [cst, s_], [p2])
            op("act", lambda h: h.activation(mean[:, 0:n], p1[:, 0:n], AF.Copy, scale=2.0), [p1], [mean])
            op("dve", lambda h: h.tensor_tensor(msq[:, 0:n], mean[:, 0:n], mean[:, 0:n], ALU.mult), [mean], [msq])
            op("dve", lambda h: h.scalar_tensor_tensor(rstd[:, 0:n], p2[:, 0:n], 2.0, msq[:, 0:n], ALU.mult, ALU.subtract), [p2, msq], [rstd])
            op("act", lambda h: h.activation(rstd[:, 0:n], rstd[:, 0:n], AF.Ln, bias=epsc[:, 1:2], scale=1.0), [rstd, epsc], [rstd])
            op("act", lambda h: h.activation(rstd[:, 0:n], rstd[:, 0:n], AF.Exp, scale=-0.5), [rstd], [rstd])
            op("dve", lambda h: h.scalar_tensor_tensor(nmr[:, 0:n], mean[:, 0:n], -1.0, rstd[:, 0:n], ALU.mult, ALU.mult), [mean, rstd], [nmr])
            for ch in range(4):
                op("dve", lambda h, ch=ch: h.tensor_tensor(acc[:, ch, 0:n], acc[:, ch, 0:n], rstd[:, 0:n], ALU.mult), [accB[ch], rstd], [accB[ch]])
                op("dve", lambda h, ch=ch: h.tensor_tensor(acc[:, ch, 0:n], acc[:, ch, 0:n], nmr[:, 0:n], ALU.add), [accB[ch], nmr], [accB[ch]])
                op("act", lambda h, ch=ch: h.activation(mix[:, 4 + ch, mixcol0:mixcol0 + n], acc[:, ch, 0:n], AF.Silu, bias=cvec[:, 2, ch:ch + 1],
                                                        scale=cvec[:, 1, ch:ch + 1]), [accB[ch], cvec], [mixB[1]])

        def qk_norm_pair(ps_, dsts):
            sets = [(t1, t2, kn), (sq[0], sq[1], msq)]
            R2 = range(2)
            for i in R2:
                a_, b_, k_ = sets[i]
                op("act", lambda h, i=i, a_=a_: h.activation(a_[:], ps_[i][:, 0:T], AF.Square), [ps_[i]], [a_])
            pm_ = [None, None]
            for i in R2:
                a_, b_, k_ = sets[i]
                pm_[i] = ps()
                op("pe", lambda h, i=i, a_=a_: h.matmul(pm_[i][:, 0:T], bones, a_[:], start=True, stop=True), [cst, a_], [pm_[i]])
            for i in R2:
                a_, b_, k_ = sets[i]
                op("act", lambda h, i=i, b_=b_: h.activation(b_[:], pm_[i][:, 0:T], AF.Ln, bias=epsc[:, 1:2], scale=1.0), [pm_[i], epsc], [b_])
            for i in R2:
                a_, b_, k_ = sets[i]
                op("act", lambda h, b_=b_: h.activation(b_[:], b_[:], AF.Exp, scale=-0.5), [b_], [b_])
            for i in R2:
                a_, b_, k_ = sets[i]
                op("dve", lambda h, i=i, b_=b_, k_=k_: h.scalar_tensor_tensor(k_[:], ps_[i][:, 0:T], qkg[:, 0:1], b_[:], ALU.mult, ALU.mult), [ps_[i], qkg, b_], [k_])
            pr_ = [None, None]
            for i in R2:
                a_, b_, k_ = sets[i]
                pr_[i] = ps()
                op("pe", lambda h, i=i, k_=k_: h.matmul(pr_[i][:, 0:T], pmt, k_[:], start=True, stop=True), [cst, k_], [pr_[i]])
            for i in R2:
                a_, b_, k_ = sets[i]
                op("pool", lambda h, a_=a_, k_=k_: h.tensor_tensor(a_[:], k_[:], cosb[:], ALU.mult), [k_, cosb], [a_])
            for i in R2:
                a_, b_, k_ = sets[i]
                op("dve", lambda h, i=i, b_=b_: h.tensor_tensor(b_[:], pr_[i][:, 0:T], sinb[:], ALU.mult), [pr_[i], sinb], [b_])
            for i in R2:
                a_, b_, k_ = sets[i]
                op("dve", lambda h, i=i, a_=a_, b_=b_: h.tensor_tensor(dsts[i], a_[:], b_[:], ALU.add), [a_, b_], [QT])

        def q_project(rope, tok0):
            s, wv = wpiece(wia_b, 0, 512)
            if rope:
                dma("sp", cosb[:], cos_d[:, tok0:tok0 + T], w=[cosb])
                dma("sp", sinb[:], sin_d[:, tok0:tok0 + T], w=[sinb])
                for c2 in range(2):
                    pp = [proj_fm(wv, s, 2 * c2), proj_fm(wv, s, 2 * c2 + 1)]
                    qk_norm_pair(pp, [QT[:, 2 * c2, :], QT[:, 2 * c2 + 1, :]])
                return
            for c in range(4):
                p = proj_fm(wv, s, c)
                qk_norm(p, 0, rope, QT[:, c, :], QT)

        for j in range(2 if stage >= 0.4 else 0):
            load_x_tile(xp_d[j * T:(j + 1) * T, :], 0, 0)
            if stage >= 0.5:
                kv_project(0, False, 0, want_out=j * T)
            if stage >= 0.6:
                q_project(False, 0)
            if stage >= 0.8:
                sa, wva = wpiece(wia_b, 768, 512)
                sgt, wvg = wpiece(wia_b, 1280, 512)
                for sq_ in range(2):
                    op("pool", lambda h: h.memset(U[:], 0.0), [], [U])
                    glu_into_U(wva, sa, wvg, sgt, 15, 256, sq_ * 256)
                    taps_p = conv_taps(0, 15, 256)

                    def between_p(hd, taps=taps_p):
                        k0, k1 = (len(taps) * hd) // 8, (len(taps) * (hd + 1)) // 8
                        for f_ in taps[k0:k1]:
                            f_()
                    for _ in attention_gen(256, sq_ * 256, [sq_ * 2, sq_ * 2 + 1], sq_ * 256, between=between_p, warm=0, LOOK=2):
                        pass
                    conv_ln(0, 15, 256, sq_ * 256, taps_done=True)
            if stage >= 0.9:
                tail(0, 0, woa_b, wf1_b[0], wf2_b[0])
            dma("pool", x1p_s[j].t.rearrange("p (k t) -> p k t", t=T), xres[:], r=[xres], w=[x1p_s[j]])

        if stage >= 2:
            dma("sp", ctm[:], ck_d.rearrange("(s p) c -> p s c", p=128), w=[ctm])
            pck = ps()
            for s_ in range(4):
                op("pe", lambda h, s_=s_: h.transpose(pck[:, s_ * 128:(s_ + 1) * 128], ctm[:, s_, :], ident), [ctm, cst], [pck], inc=(s_ == 3))
            op("act", lambda h: h.copy(KT[:, 4096:4608], pck[:, 0:512]), [pck], [KT])
            dma("sp", vst[:], cv_d.rearrange("(s p) c -> p s c", p=128), w=[vst])
            op("dve", lambda h: h.tensor_copy(VA[:, 32:36, :, 0:64], vst[:].rearrange("p s (k d) -> p s k d", d=64)), [vst], [VA])
            hbufs = [(hb, [hb]), (mix, [mixB[0], mixB[1]])]
            load_x_tile(xs_d[0:T, :], 0, 1, keep_x=False, hdst=hbufs[0])
            for j in range(8):
                if j < 7:
                    load_x_tile(xs_d[(j + 1) * T:(j + 2) * T, :], 0, 1, keep_x=False, hdst=hbufs[(j + 1) % 2])
                kv_project(j * T, True, j * T, hsrc=hbufs[j % 2])
            xresB = sb("xresB", [128, 8, T], F32, l0)
            xres.set([xres0, xresB])
            op("pool", lambda h: h.memset(U[:], 0.0), [], [U])

            def stage_A(j):
                load_x_tile(xs_d[j * T:(j + 1) * T, :], 0, 1)
                if j < 7:
                    stg = big.t[0:16, 4096:5120]
                    dma("sp", stg, xs_d[(j + 1) * T:(j + 1) * T + 16, :], w=[big])
                    ph = ps()
                    for kc in range(8):
                        op("pe", lambda h, kc=kc: h.transpose(ph[:, kc * 16:(kc + 1) * 16], stg[:, kc * 128:(kc + 1) * 128], ident[0:16, 0:16]), [big, cst], [ph], inc=(kc == 7))
                    for kc in range(8):
                        op("act", lambda h, kc=kc: h.activation(hb[:, kc, T:T + 16], ph[:, kc * 16:(kc + 1) * 16], AF.Identity,
                                                                bias=dv[:, 0, 1, 1, kc:kc + 1], scale=dv[:, 0, 1, 0, kc:kc + 1]), [ph, dv], [hb])
                q_project(True, j * T)
                sa, wva = wpiece(wia_b, 768, 512)
                sgt, wvg = wpiece(wia_b, 1280, 512)
                if j > 0:
                    op("pool", lambda h: h.tensor_copy(U[:, :, 0:15], U[:, :, T:T + 15]), [U], [U])
                glu_into_U(wva, sa, wvg, sgt, 15, T, 0)
                if j < 7:
                    glu_into_U(wva, sa, wvg, sgt, 15 + T, 16, T)
                else:
                    op("pool", lambda h: h.memset(U[:, :, 15 + T:15 + T + 16], 0.0), [], [U])

            def make_attn(warm):
                taps = conv_taps(1, 15, T)

                def between(hd, taps=taps):
                    k0, k1 = (len(taps) * hd) // 8, (len(taps) * (hd + 1)) // 8
                    for f_ in taps[k0:k1]:
                        f_()
                return attention_gen(T, 0, list(range(36)), 0, between=between, warm=warm, LOOK=(3 if warm else 2))

            xres.i = 0
            stage_A(0)
            for _ in make_attn(512):
                pass
            conv_ln(1, 15, T, 0, taps_done=True)
            for j in range(8):
                b_ = j % 2
                xres.i = b_
                tail_outproj(0, 1, woa_b)
                if j < 7:
                    xres.i = 1 - b_
                    stage_A(j + 1)
                    xres.i = b_
                tail_ln1(0, 1)
                fg = ffn_gen(0, 1, wf1_b[0], wf2_b[0])
                if j < 7:
                    ps_n[0] = 2
                    ag = make_attn(0)
                    a_done = f_done = False
                    while not (a_done and f_done):
                        if not a_done:
                            try:
                                next(ag)
                            except StopIteration:
                                a_done = True
                        if not f_done:
                            try:
                                next(fg)
                            except StopIteration:
                                f_done = True
                    ps_n[0] = 5
                else:
                    for _ in fg:
                        pass
                tail_ln2(0, 1)
                dma("pool", x1s_s[j].t.rearrange("p (k t) -> p k t", t=T), xres[:], r=[xres], w=[x1s_s[j]])
                if j < 7:
                    conv_ln(1, 15, T, 0, taps_done=True)
            xres.i = 0
            xres.set([xres0])

        fw.barrier()
        l0.close()

        if stage >= 3:
            l1 = ExitStack()
            mhg = sb("mhg", [128, D], F32, l1)
            dma("sp", mhg[:], mhg_d, w=[mhg])
            qT = sb("qT", [128, 8, T], BF16, l1)
            kT = sb("kT", [128, 8, T], BF16, l1)
            ktm = sb("ktm", [128, 4, D], BF16, l1)
            vaug = sb("vaug", [128, 4, 4, 258], BF16, l1)
            kw = [sb("kw%d" % i, [128, 256], BF16, l1) for i in range(4)]
            smt = [sb("smt%d" % i, [128, 128], BF16, l1) for i in range(4)]
            cst8 = sb("cstate", [128, 2, 4, 2, 258], F32, l1)
            cstB = [[B(cst8.t[:, d_, h_]) for h_ in range(4)] for d_ in range(2)]
            cdb = [sb("cdb%d" % i, [128, 2, 258], BF16, l1) for i in range(4)]
            hacc_t = l1.enter_context(nc.sbuf_tensor("sb_hacc", [128, 4, D], F32))
            hacc = [B(hacc_t[:, c, :]) for c in range(4)]
            haccB = [[B(hacc_t[:, c, h_ * 256:(h_ + 1) * 256]) for h_ in range(4)] for c in range(4)]
            hacc_all = [haccB[c][h_] for c in range(4) for h_ in range(4)]
            gsig = sb("gsig", [128, 4, D], BF16, l1)
            gl = {}
            for gi, k_ in enumerate(("li", "pf", "ab", "lf", "P", "P2", "ws", "cl")):
                gl[k_] = B(big.t[0:4, 4096 + gi * T:4096 + (gi + 1) * T])
                gl[k_].base = 0
            gcar = sb("gcar", [4, 2, 2], F32, l1)
            gdec = sb("gdec", [4, 8], F32, l1)
            gdd = sb("gdd", [4, 4], F32, l1)
            gbd = sb("gbd", [4, 4, 4], F32, l1)
            decb = sb("decb", [128, 16], F32, l1)
            wct = sb("wct", [128, 2, 4, 4], F32, l1)
            dn = sb("dn", [128, 4], F32, l1)
            dnB = [B(dn.t[:, h_:h_ + 1]) for h_ in range(4)]
            ssq = sb("ssq", [128, 16], F32, l1)
            junk = sb("junk", [128, 512], F32, l1)
            mout = sb("mout", [4, 1], F32, l1)
            cn = [0]

            op("pool", lambda h: h.memset(vaug[:, :, :, 256:258], 1.0), [], [vaug])
            for t_ in (gdd, gdec, wct, decb, gbd, dn):
                op("pool", lambda h, t_=t_: h.memset(t_[:], 0.0), [], [t_])
            for t_ in gl.values():
                op("dve", lambda h, t_=t_: h.memset(t_[:], 0.0), [], [t_])

            def tm_proj(wv, s, s_):
                p = ps()
                for kc in range(8):
                    op("pe", lambda h, kc=kc: h.matmul(p[:, 0:512], hb[:, kc, s_ * 128:(s_ + 1) * 128], wv[:, kc, :],
                                                       start=(kc == 0), stop=(kc == 7)), [s, hb], [p], inc=(kc == 7))
                return p

            def in_proj(want_q, want_o):
                if want_q:
                    for half in range(2):
                        s, wv = wpiece(wim_b, half * 512, 512)
                        for jj in range(4):
                            p = proj_fm(wv, s, jj)
                            op("act", lambda h, ch=half * 4 + jj, p=p: h.copy(qT[:, ch, :], p[:, 0:T]), [p], [qT])
                for half in range(2):
                    s, wv = wpiece(wim_b, D + half * 512, 512)
                    if want_q:
                        for jj in range(4):
                            p = proj_fm(wv, s, jj)
                            op("dve", lambda h, ch=half * 4 + jj, p=p: h.tensor_scalar(kT[:, ch, :], p[:, 0:T], 0.0625, None, ALU.mult), [p], [kT])
                    for s_ in range(4):
                        p = tm_proj(wv, s, s_)
                        op("act", lambda h, s_=s_, p=p, half=half: h.activation(ktm[:, s_, half * 512:(half + 1) * 512], p[:, 0:512], AF.Copy, scale=0.0625), [p], [ktm])
                for half in range(2):
                    s, wv = wpiece(wim_b, 2 * D + half * 512, 512)
                    for s_ in range(4):
                        p = tm_proj(wv, s, s_)
                        op("dve", lambda h, s_=s_, p=p, half=half: h.tensor_copy(vaug[:, s_, half * 2:half * 2 + 2, 0:256],
                                                                                 p[:, 0:512].rearrange("p (a b) -> p a b", b=256)), [p], [vaug])
                if want_o:
                    for half in range(2):
                        s, wv = wpiece(wim_b, 3 * D + half * 512, 512)
                        for s_ in range(4):
                            p = tm_proj(wv, s, s_)
                            op("act", lambda h, p=p: h.activation(junk[:, 0:512], p[:, 0:512], AF.Sigmoid), [p], [junk])
                            op("dve", lambda h, s_=s_, half=half: h.tensor_tensor(gsig[:, s_, half * 512:(half + 1) * 512], junk[:, 0:512],
                                                                                  mhg[:, half * 512:(half + 1) * 512], ALU.mult), [junk, mhg], [gsigB[s_]])

            def gates(which, d, asc, t0, n):
                g0 = which * 16 + d * 8
                cs = slice(t0, t0 + n)
                c0 = t0 // 128
                ncn_ = n // 128
                for nm, off in (("li", 0), ("pf", 4)):
                    p = ps()
                    for kc in range(8):
                        op("pe", lambda h, kc=kc, off=off: h.matmul(p[0:4, 0:n], wgs[:, kc, g0 + off:g0 + off + 4], hb[:, kc, cs], start=(kc == 0), stop=(kc == 7)),
                           [wgs, hb], [p], inc=(kc == 7))
                    bi = d * 2 + (0 if nm == "li" else 1)
                    op("act", lambda h, nm=nm, p=p, bi=bi: h.activation(gl[nm][:, cs], p[0:4, 0:n], AF.Identity, bias=bg[:, which, bi:bi + 1], scale=1.0), [p, bg], [gl[nm], big])
                li, pf, ab, lf = gl["li"], gl["pf"], gl["ab"], gl["lf"]
                op("act", lambda h: h.activation(ab[:, cs], pf[:, cs], AF.Abs), [pf], [ab])
                op("act", lambda h: h.activation(ab[:, cs], ab[:, cs], AF.Exp, scale=-1.0), [ab], [ab])
                op("act", lambda h: h.activation(ab[:, cs], ab[:, cs], AF.Ln, bias=epsc[0:4, 2:3], scale=1.0), [ab, epsc], [ab])
                op("dve", lambda h: h.scalar_tensor_tensor(lf[:, cs], pf[:, cs], 0.0, ab[:, cs], ALU.min, ALU.subtract), [pf, ab], [lf])

                def scan(a, b, opx):
                    sh = 1
                    while sh < n:
                        if asc:
                            op("dve", lambda h, a=a, b=b, sh=sh: h.tensor_tensor(b[:, t0 + sh:t0 + n], a[:, t0 + sh:t0 + n], a[:, t0:t0 + n - sh], opx), [a], [b])
                            op("dve", lambda h, a=a, b=b, sh=sh: h.tensor_copy(b[:, t0:t0 + sh], a[:, t0:t0 + sh]), [a], [b])
                        else:
                            op("dve", lambda h, a=a, b=b, sh=sh: h.tensor_tensor(b[:, t0:t0 + n - sh], a[:, t0:t0 + n - sh], a[:, t0 + sh:t0 + n], opx), [a], [b])
                            op("dve", lambda h, a=a, b=b, sh=sh: h.tensor_copy(b[:, t0 + n - sh:t0 + n], a[:, t0 + n - sh:t0 + n]), [a], [b])
                        a, b = b, a
                        sh *= 2
                    return a, b
                Fl, spare = scan(lf, ab, ALU.add)
                op("dve", lambda h: h.tensor_scalar(Fl[:, cs], Fl[:, cs], gcar[:, d, 0:1], None, ALU.add), [Fl, gcar], [Fl])
                r_ = pf
                op("dve", lambda h: h.tensor_tensor(r_[:, cs], li[:, cs], Fl[:, cs], ALU.subtract), [li, Fl], [r_])
                op("dve", lambda h: h.tensor_copy(gl["P"][:, cs], r_[:, cs]), [r_], [gl["P"]])
                Pl, _ = scan(gl["P"], gl["P2"], ALU.max)
                op("dve", lambda h: h.tensor_scalar(Pl[:, cs], Pl[:, cs], gcar[:, d, 1:2], None, ALU.max), [Pl, gcar], [Pl])
                view = lambda x: x[:, cs].rearrange("p (c t) -> p c t", t=128)
                e_off = 127 if asc else 0
                pe_sl = slice(4 + c0, 4 + c0 + ncn_)
                ps_sl = slice(c0, c0 + ncn_)
                op("dve", lambda h: h.tensor_copy(gdec[:, pe_sl], view(Pl)[:, :, e_off]), [Pl], [gdec])
                if asc:
                    if ncn_ > 1:
                        op("dve", lambda h: h.tensor_copy(gdec[:, c0 + 1:c0 + ncn_], gdec[:, 4 + c0:4 + c0 + ncn_ - 1]), [gdec], [gdec])
                    op("dve", lambda h: h.tensor_copy(gdec[:, c0:c0 + 1], gcar[:, d, 1:2]), [gcar, gdec], [gdec])
                else:
                    if ncn_ > 1:
                        op("dve", lambda h: h.tensor_copy(gdec[:, c0:c0 + ncn_ - 1], gdec[:, 4 + c0 + 1:4 + c0 + ncn_]), [gdec], [gdec])
                    op("dve", lambda h: h.tensor_copy(gdec[:, c0 + ncn_ - 1:c0 + ncn_], gcar[:, d, 1:2]), [gcar, gdec], [gdec])
                op("dve", lambda h: h.tensor_tensor(gdd[:, ps_sl], gdec[:, ps_sl], gdec[:, pe_sl], ALU.subtract), [gdec], [gdd])
                op("act", lambda h: h.activation(gdd[:, ps_sl], gdd[:, ps_sl], AF.Exp), [gdd], [gdd])
                pend_b = gdec[:, pe_sl].unsqueeze(2).to_broadcast([4, ncn_, 128])
                e1 = li
                op("dve", lambda h: h.tensor_tensor(view(e1), view(r_), pend_b, ALU.subtract), [r_, gdec], [e1])
                op("act", lambda h: h.activation(gl["ws"][:, cs], e1[:, cs], AF.Exp), [e1], [gl["ws"]])
                op("dve", lambda h: h.scalar_tensor_tensor(view(e1), view(Fl), -1.0, pend_b, ALU.mult, ALU.subtract), [Fl, gdec], [e1])
                op("act", lambda h: h.activation(gl["cl"][:, cs], e1[:, cs], AF.Exp), [e1], [gl["cl"]])
                last = t0 + n - 1 if asc else t0
                op("dve", lambda h: h.tensor_copy(gcar[:, d, 0:1], Fl[:, last:last + 1]), [Fl, gcar], [gcar])
                op("dve", lambda h: h.tensor_copy(gcar[:, d, 1:2], Pl[:, last:last + 1]), [Pl, gcar], [gcar])
                pw = ps()
                for qi, nm in enumerate(("ws", "cl")):
                    for s_ in range(c0, c0 + ncn_):
                        last_ = (qi == 1 and s_ == c0 + ncn_ - 1)
                        op("pe", lambda h, qi=qi, nm=nm, s_=s_: h.transpose(pw[:, (qi * 4 + s_) * 4:(qi * 4 + s_) * 4 + 4], gl[nm][:, s_ * 128:(s_ + 1) * 128], ident[gl[nm].base:gl[nm].base + 4, gl[nm].base:gl[nm].base + 4]),
                           [gl[nm], cst], [pw], inc=last_)
                for qi in range(2):
                    op("act", lambda h, qi=qi: h.copy(wct[:, qi, c0:c0 + ncn_, :], pw[:, (qi * 4 + c0) * 4:(qi * 4 + c0 + ncn_) * 4].rearrange("p (b c) -> p b c", c=4)), [pw], [wct])
                op("dve", lambda h: h.tensor_tensor(gbd[:], gdd[:].unsqueeze(1).to_broadcast([4, 4, 4]), c4[:, 0:4].unsqueeze(2).to_broadcast([4, 4, 4]), ALU.mult),
                   [gdd, c4], [gbd])
                pd = ps()
                op("pe", lambda h: h.matmul(pd[:, 0:16], c4[:, 4:132], gbd[:].rearrange("p a b -> p (a b)"), start=True, stop=True), [c4, gbd], [pd])
                op("act", lambda h: h.copy(decb[:], pd[:, 0:16]), [pd], [decb])

            def mlstm_chunk(d, asc, c, with_out, first_dir):
                mk = maskb[:, 0 if asc else 1, :]
                H = range(4)
                ws_ap = [wct[:, 0, c, hd:hd + 1] for hd in H]
                cl_ap = [wct[:, 1, c, hd:hd + 1] for hd in H]
                dec_ap = [decb[:, hd * 4 + c:hd * 4 + c + 1] for hd in H]
                hsl = [slice(hd * 256, (hd + 1) * 256) for hd in H]
                cs_ = slice(c * 128, (c + 1) * 128)
                for hd in H:
                    op("act", lambda h, hd=hd: h.activation(kw[hd][:], ktm[:, c, hsl[hd]], AF.Copy, scale=ws_ap[hd]), [ktm, wct], [kw[hd]])
                pN = [None] * 4
                if with_out:
                    pS = [None] * 4
                    for hd in H:
                        pS[hd] = ps()
                        for dk in range(2):
                            op("pe", lambda h, dk=dk, hd=hd: h.matmul(pS[hd][:, 0:128], kT[:, 2 * hd + dk, cs_], qT[:, 2 * hd + dk, cs_],
                                                                      start=(dk == 0), stop=(dk == 1)), [kT, qT], [pS[hd]], inc=(dk == 1))
                    for hd in H:
                        op("dve", lambda h, hd=hd: h.scalar_tensor_tensor(smt[hd][:], pS[hd][:, 0:128], ws_ap[hd], mk, ALU.mult, ALU.mult), [pS[hd], wct, maskb], [smt[hd]])
                    for hd in H:
                        op("act", lambda h, hd=hd: h.activation(cdb[hd][:, :, 0:257], cst8[:, d, hd, :, 0:257], AF.Copy, scale=dec_ap[hd]), [cstB[d][hd], decb], [cdb[hd]])
                    for hd in H:
                        pN[hd] = ps()
                        op("pe", lambda h, hd=hd: h.matmul(pN[hd][:, 0:257], smt[hd][:], vaug[:, c, hd, 0:257], start=True, stop=False), [smt[hd], vaug], [pN[hd]], inc=False)
                        for dk in range(2):
                            op("pe", lambda h, dk=dk, hd=hd: h.matmul(pN[hd][:, 0:257], qT[:, 2 * hd + dk, cs_], cdb[hd][:, dk, 0:257], start=False, stop=(dk == 1)),
                               [qT, cdb[hd]], [pN[hd]], inc=(dk == 1))
                    for hd in H:
                        op("act", lambda h, hd=hd: h.activation(dn[:, hd:hd + 1], pN[hd][:, 256:257], AF.Abs), [pN[hd]], [dnB[hd]])
                    for hd in H:
                        op("dve", lambda h, hd=hd: h.tensor_tensor(dn[:, hd:hd + 1], dn[:, hd:hd + 1], cl_ap[hd], ALU.max), [dnB[hd], wct], [dnB[hd]])
                        op("dve", lambda h, hd=hd: h.reciprocal(dn[:, hd:hd + 1], dn[:, hd:hd + 1]), [dnB[hd]], [dnB[hd]])
                    for hd in H:
                        if first_dir:
                            op("act", lambda h, hd=hd: h.activation(hacc_t[:, c, hsl[hd]], pN[hd][:, 0:256], AF.Copy, scale=dn[:, hd:hd + 1]), [pN[hd], dnB[hd]], [haccB[c][hd]])
                        else:
                            op("dve", lambda h, hd=hd: h.scalar_tensor_tensor(hacc_t[:, c, hsl[hd]], pN[hd][:, 0:256], dn[:, hd:hd + 1], hacc_t[:, c, hsl[hd]],
                                                                              ALU.mult, ALU.add), [pN[hd], dnB[hd], haccB[c][hd]], [haccB[c][hd]])
                for hd in H:
                    pC = [ps(), ps()]
                    for dk in range(2):
                        op("pe", lambda h, dk=dk, hd=hd: h.matmul(pC[dk][:, 0:257], kw[hd][:, dk * 128:(dk + 1) * 128], vaug[:, c, hd, 0:257], start=True, stop=True),
                           [kw[hd], vaug], [pC[dk]])
                    for dk in range(2):
                        op("dve", lambda h, dk=dk, hd=hd: h.scalar_tensor_tensor(cst8[:, d, hd, dk, 0:257], cst8[:, d, hd, dk, 0:257], dec_ap[hd], pC[dk][:, 0:257], ALU.mult, ALU.add),
                           [cstB[d][hd], decb, pC[dk]], [cstB[d][hd]])

            ssqB = [B(ssq.t[:, c_ * 4:(c_ + 1) * 4]) for c_ in range(4)]
            gsigB = [B(gsig.t[:, c_, :]) for c_ in range(4)]
            op("pool", lambda h: h.memset(ssq[:], 0.0), [], ssqB)

            def finish_chunk(c):
                for hh in range(2):
                    op("act", lambda h, hh=hh: h.activation(junk[:], hacc_t[:, c, hh * 512:(hh + 1) * 512], AF.Square), haccB[c], [junk])
                    op("dve", lambda h, hh=hh: h.reduce_sum(ssq[:, c * 4 + hh * 2:c * 4 + hh * 2 + 2], junk[:].rearrange("p (a b) -> p a b", b=256), mybir.AxisListType.X), [junk], [ssqB[c]])
                op("act", lambda h: h.activation(ssq[:, c * 4:(c + 1) * 4], ssq[:, c * 4:(c + 1) * 4], AF.Sqrt, bias=epsc[:, 1:2], scale=1.0 / 256.0), [ssqB[c], epsc], [ssqB[c]])
                op("dve", lambda h: h.reciprocal(ssq[:, c * 4:(c + 1) * 4], ssq[:, c * 4:(c + 1) * 4]), [ssqB[c]], [ssqB[c]])
                for hd in range(4):
                    hsl = slice(hd * 256, (hd + 1) * 256)
                    op("dve", lambda h, hd=hd, hsl=hsl: h.scalar_tensor_tensor(gsig[:, c, hsl], hacc_t[:, c, hsl], ssq[:, c * 4 + hd:c * 4 + hd + 1], gsig[:, c, hsl],
                                                                             ALU.mult, ALU.mult), [haccB[c][hd], ssqB[c], gsigB[c]], [gsigB[c]])
                for kc in range(8):
                    op("pe", lambda h, kc=kc: h.transpose(pstb[:, kc * 128:(kc + 1) * 128], gsig[:, c, kc * 128:(kc + 1) * 128], identb[:]), [gsigB[c], identb], [pstb], inc=(kc == 7))
                op("act", lambda h: h.copy(mix[:, :, c * 128:(c + 1) * 128], pstb[:, 0:1024].rearrange("p (k t) -> p k t", t=128)), [pstb], [mixB[0], mixB[1]])

            def load_x1(scr, r):
                dma("sp", xres[:], scr.t.rearrange("p (k t) -> p k t", t=T), r=[scr], w=[xres])
                for kc in range(8):
                    op("act", lambda h, kc=kc: h.activation(hb[:, kc, 0:T], xres[:, kc, :], AF.Identity, bias=dv[:, 1, r, 1, kc:kc + 1], scale=dv[:, 1, r, 0, kc:kc + 1]),
                       [(xres, kc), dv], [hb])

            def init_state(d, zero, src_dir=None):
                if zero:
                    op("pool", lambda h: h.memset(cst8[:, d], 0.0), [], cstB[d])
                    op("pool", lambda h: h.memset(gcar[:, d, :], 0.0), [], [gcar])
                else:
                    for hd in range(4):
                        dma("sp", cst8[:, d, hd, :, 0:256], stc_d[src_dir, hd].rearrange("(k p) v -> p k v", p=128), w=[cstB[d][hd]])
                    dma("sp", cst8[:, d, :, :, 256:257], stn_d[:, src_dir], w=cstB[d], allow_slow_non_contiguous=True)
                    op("pool", lambda h: h.memset(gcar[:, d, 0:1], 0.0), [], [gcar])
                    dma("sp", gcar[:, d, 1:2], stm_d[:, src_dir:src_dir + 1], w=[gcar], allow_slow_non_contiguous=True)

            for j in range(2):
                load_x1(x1p_s[j], 0)
                in_proj(True, True)
                for sq_ in range(2):
                    seq = j * 2 + sq_
                    for d, asc in ((0, True), (1, False)):
                        init_state(d, True)
                        gates(0, d, asc, sq_ * 256, 256)
                        order = [sq_ * 2, sq_ * 2 + 1] if asc else [sq_ * 2 + 1, sq_ * 2]
                        for c in order:
                            mlstm_chunk(d, asc, c, True, d == 0)
                            if d == 1:
                                finish_chunk(c)
                        for hd in range(4):
                            dma("pool", ncc_d[seq, d, hd].rearrange("(k p) v -> p k v", p=128), cst8[:, d, hd, :, 0:256], r=[cstB[d][hd]])
                        dma("pool", ncn_d[seq, d].rearrange("h (k p) o -> p h k o", p=128), cst8[:, d, :, :, 256:257], r=cstB[d], allow_slow_non_contiguous=True)
                        op("dve", lambda h, d=d: h.tensor_tensor(mout[:], gcar[:, d, 0:1], gcar[:, d, 1:2], ALU.add), [gcar], [mout])
                        dma("pool", ncm_d[seq, d].rearrange("(h o) -> h o", o=1), mout[:], r=[mout], allow_slow_non_contiguous=True)
                tail(1, 0, wom_b, wf1_b[1], wf2_b[1])
                store_tokens(yp_d[j * T:(j + 1) * T, :])

            if stage >= 4:
                init_state(0, False, 0)
                for j in range(4):
                    load_x1(x1s_s[j], 1)
                    gates(1, 0, True, 0, T)
                    in_proj(True, False)
                    for c in range(4):
                        mlstm_chunk(0, True, c, True, True)
                    dma("pool", hf_s[j].t.rearrange("p (c f) -> p c f", f=D), hacc_t[:], r=hacc_all, w=[hf_s[j]])
                init_state(1, False, 1)
                for j in range(7, -1, -1):
                    own = j < 4
                    load_x1(x1s_s[j], 1)
                    gates(1, 1, False, 0, T)
                    in_proj(own, own)
                    if own:
                        dma("sp", hacc_t[:], hf_s[j].t.rearrange("p (c f) -> p c f", f=D), r=[hf_s[j]], w=hacc_all)
                    for c in range(3, -1, -1):
                        mlstm_chunk(1, False, c, own, False)
                        if own:
                            finish_chunk(c)
                    if own:
                        tail(1, 1, wom_b, wf1_b[1], wf2_b[1])
                        store_tokens(ys_d[j * T:(j + 1) * T, :])
            l1.close()

        toks = fw.all_tokens()
        for e in ("sp", "pool", "act", "dve", "pe"):
            fw._wait(e, toks)
    return nc


_PROG = {}


def _host_inputs(inp):
    f = lambda a: np.ascontiguousarray(np.asarray(a, dtype=np.float32))
    x_prompt, x_sample = f(inp["x_prompt"]), f(inp["x_sample"])
    cache_k, cache_v = f(inp["cache_k"]), f(inp["cache_v"])
    state_c, state_n, state_m = f(inp["state_c"]), f(inp["state_n"]), f(inp["state_m"])
    c, c_ctx = f(inp["c"]), f(inp["c_ctx"])
    pk = lambda v: np.ascontiguousarray(v.reshape(-1, 128).T)
    b_ada = f(inp["b_ada"])
    bada_l = np.stack([pk(b_ada[l]) for l in range(2)], axis=1)
    ln_g, ln_b = f(inp["ln_g"]), f(inp["ln_b"])
    lng_l = np.stack([np.stack([pk(ln_g[l, w]) for w in range(2)], axis=1) for l in range(2)], axis=1)
    lnb_l = np.stack([np.stack([pk(ln_b[l, w]) for w in range(2)], axis=1) for l in range(2)], axis=1)
    w_in_a = f(inp["w_in_a"])[0]
    w_out_a = f(inp["w_out_a"])[0]
    perm = np.zeros(512, dtype=np.int64)
    for cc in range(4):
        for g in range(2):
            perm[cc * 128 + g * 64:cc * 128 + (g + 1) * 64] = (4 * g + cc) * 64 + np.arange(64)
    wia = w_in_a.copy()
    wia[:, 0:512] = w_in_a[:, perm]
    woa = w_out_a.copy()
    woa[0:512, :] = w_out_a[perm, :]
    qg, kg = f(inp["q_gain"])[0], f(inp["k_gain"])[0]
    qkg = np.stack([np.tile(qg, 2), np.tile(kg, 2)], axis=1)
    conv_w = f(inp["conv_w"])[0]
    cwn = np.ascontiguousarray(conv_w.T.reshape(4, 128, 31).transpose(1, 0, 2))
    cwr = np.ascontiguousarray(cwn[:, :, ::-1])
    cvec = np.stack([pk(f(inp["conv_b"])[0]), pk(f(inp["conv_ln_g"])[0]), pk(f(inp["conv_ln_b"])[0])], axis=1)
    w_in_m = f(inp["w_in_m"])[0]
    wim = np.ascontiguousarray(w_in_m[:, 0:4096])
    wgate = w_in_m[:, 4096:4112]
    b_gate = f(inp["b_gate_m"])[0]
    bgm = b_gate.reshape(4, 4).T
    mhg = np.ascontiguousarray(np.broadcast_to(f(inp["mh_gain"])[0][None, :], (128, 1024)))
    n = 4096
    row = (np.arange(n) // 64).astype(np.float32)
    col = (np.arange(n) % 64).astype(np.float32)
    freqs = (np.float32(10000.0) ** (-np.arange(0, 32, 2, dtype=np.float32) / np.float32(32))).astype(np.float32)
    ang = np.concatenate([row[:, None] * freqs, col[:, None] * freqs], axis=-1).astype(np.float32)
    cos32, sin32 = np.cos(ang), np.sin(ang)
    cosT = np.tile(np.repeat(cos32, 2, axis=1).T, (2, 1)).astype(np.float32)
    sinT = np.tile(np.repeat(sin32, 2, axis=1).T, (2, 1)).astype(np.float32)
    cstm = np.zeros((128, 6, 128), np.float32)
    cstm[:, 0, :] = np.eye(128)
    cstm[0:64, 1, 0:64] = 1.0 / 64
    cstm[64:128, 1, 64:128] = 1.0 / 64
    cstm[:, 2, :] = 1.0 / 1024
    pm = np.zeros((128, 128), np.float32)
    for i in range(64):
        pm[2 * i, 2 * i + 1] = -1.0
        pm[2 * i + 1, 2 * i] = 1.0
    cstm[:, 3, :] = pm.T
    s_ = np.arange(128)[:, None]
    t_ = np.arange(128)[None, :]
    cstm[:, 4, :] = (s_ <= t_)
    cstm[:, 5, :] = (s_ >= t_)
    c4 = np.zeros((4, 132), np.float32)
    c4[:, 0:4] = np.eye(4)
    c4[:, 4:132] = 1.0
    shared = {
        "w_ada": f(inp["w_ada"]), "b_ada": bada_l, "ln_g": lng_l, "ln_b": lnb_l,
        "w_in_a": wia, "w_out_a": woa, "qkg": qkg, "cvec": cvec,
        "w_ff1": f(inp["w_ff1"]), "w_ff2": f(inp["w_ff2"]), "w_in_m": wim, "mhg": mhg,
        "w_out_m": f(inp["w_out_m"])[0], "cst": cstm, "c4": c4,
    }
    maps = []
    for r in range(8):
        s, p = r // 2, r % 2
        m = dict(shared)
        m["xp"] = np.ascontiguousarray(x_prompt[4 * r:4 * r + 4].reshape(1024, 1024))
        xs = x_sample[s]
        m["xs"] = np.ascontiguousarray(xs[::-1]) if p else xs
        m["cosT"] = np.ascontiguousarray(cosT[:, ::-1]) if p else cosT
        m["sinT"] = np.ascontiguousarray(sinT[:, ::-1]) if p else sinT
        m["ck"] = np.ascontiguousarray(cache_k[s, 0].reshape(512, 128))
        m["cv"] = np.ascontiguousarray(cache_v[s, 0].reshape(512, 128))
        dirs = [1, 0] if p else [0, 1]
        m["stc"] = np.ascontiguousarray(state_c[s, 0][dirs])
        stn = state_n[s, 0][dirs]
        m["stn"] = np.ascontiguousarray(stn.reshape(2, 4, 2, 128).transpose(3, 0, 1, 2))[..., None]
        m["stm"] = np.ascontiguousarray(state_m[s, 0][dirs].T)
        m["cond"] = np.ascontiguousarray(np.stack([pk(c_ctx), pk(c[s])], axis=2))
        m["conv_w"] = np.ascontiguousarray(np.stack([cwn, cwr if p else cwn], axis=1))
        gp = wgate
        gs = np.concatenate([wgate[:, 8:16], wgate[:, 0:8]], axis=1) if p else wgate
        m["w_g"] = np.ascontiguousarray(np.concatenate([gp, gs], axis=1))
        bgs = np.concatenate([bgm[:, 2:4], bgm[:, 0:2]], axis=1) if p else bgm
        m["b_g"] = np.ascontiguousarray(np.stack([bgm, bgs], axis=1))
        maps.append(m)
    return maps


def kernel(**inputs):
    if "nc" not in _PROG:
        _PROG["nc"] = build_program()
    nc = _PROG["nc"]
    maps = _host_inputs(inputs)
    res = run_bass_kernel_spmd(nc, maps, core_ids=list(range(8)))
    R = res.results
    y_prompt = np.concatenate([R[r]["yp"].reshape(4, 256, 1024) for r in range(8)], axis=0)
    y_sample = np.zeros((4, 4096, 1024), np.float32)
    for r in range(8):
        s, p = r // 2, r % 2
        ys = R[r]["ys"]
        if p:
            y_sample[s, 2048:4096] = ys[::-1]
        else:
            y_sample[s, 0:2048] = ys
    nk = np.concatenate([R[r]["nk"].reshape(4, 1, 256, 2, 64) for r in range(8)], axis=0)
    nv = np.concatenate([R[r]["nv"].reshape(4, 1, 256, 2, 64) for r in range(8)], axis=0)
    ncc = np.concatenate([R[r]["ncc"].reshape(4, 1, 2, 4, 256, 256) for r in range(8)], axis=0)
    ncn = np.concatenate([R[r]["ncn"].reshape(4, 1, 2, 4, 256) for r in range(8)], axis=0)
    ncm = np.concatenate([R[r]["ncm"].reshape(4, 1, 2, 4) for r in range(8)], axis=0)
    return (y_prompt.astype(np.float32), y_sample, nk.astype(np.float32), nv.astype(np.float32),
            ncc.astype(np.float32), ncn.astype(np.float32), ncm.astype(np.float32))
```
